# Optimizing a Trainium2 kernel written in Bass

```python
import jax, jax.numpy as jnp
from jax import lax
import numpy as np

D_MODEL = 1024
BATCH = 8
SEQ = 2048
DEPTH = 2
DEC_BATCH = 128
DEC_SEQ = 1
PAST_LEN = 16384
PAGE_SIZE = 128

CHUNK = 128
D_CM = D_MODEL
CM_GROUPS = 8
CM_GROUP_DIM = D_CM // CM_GROUPS
EXPAND = 2
D_INNER = EXPAND * D_MODEL
SSD_HEAD_DIM = 64
SSD_HEADS = D_INNER // SSD_HEAD_DIM
SSD_GROUPS = 4
SSD_STATE = 128
CONV_K = 4
CONV_DIM = D_INNER + 2 * SSD_GROUPS * SSD_STATE
D_FF = ((8 * D_MODEL + 3 * 256 - 1) // (3 * 256)) * 256
IN_COLS = 2 * D_CM + D_INNER + CONV_DIM + SSD_HEADS + 2 * D_MODEL
ALPHA = (2 * DEPTH) ** 0.25
BETA = (8 * DEPTH) ** -0.25
LN_EPS = 1e-5

kernel_name = "gated_chunkmlp_ssd_deepnorm_adaln_step"


def layer_norm(x, g, b):
    xf = x.astype(jnp.float32)
    mu = jnp.mean(xf, axis=-1, keepdims=True)
    var = jnp.mean(jnp.square(xf - mu), axis=-1, keepdims=True)
    y = (xf - mu) * lax.rsqrt(var + LN_EPS)
    return (y * g.astype(jnp.float32) + b.astype(jnp.float32)).astype(x.dtype)


def rms_norm(x, g):
    xf = x.astype(jnp.float32)
    y = xf * lax.rsqrt(jnp.mean(jnp.square(xf), axis=-1, keepdims=True) + LN_EPS)
    return (y * g.astype(jnp.float32)).astype(x.dtype)


def causal_dwconv(xp, w, b):
    L = xp.shape[1] - CONV_K + 1
    out = b
    for k in range(CONV_K):
        out = out + xp[:, k:k + L] * w[k]
    return out


def chunk_spatial_mix(v, w_s, b_s):
    Bsz, L, _ = v.shape
    ch = min(CHUNK, L)
    nc = L // ch
    mask = jnp.tril(jnp.ones((ch, ch), dtype=bool))
    w = jnp.where(mask[None], w_s[:, :ch, :ch], 0.0).astype(v.dtype)
    bias = b_s[:, :ch].astype(v.dtype)
    v4 = v.reshape(Bsz, nc, ch, CM_GROUPS, CM_GROUP_DIM)
    out = jnp.einsum('gij,bcjgd->bcigd', w, v4) + bias.T[None, None, :, :, None]
    return out.reshape(Bsz, L, D_CM)


def ssd_scan(x, dt, a, bm, cm, h0):
    Bsz, L, H, P = x.shape
    G, N = SSD_GROUPS, SSD_STATE
    E = H // G
    ch = min(CHUNK, L)
    nc = L // ch
    f32 = jnp.float32
    xdt = (x.astype(f32) * dt[..., None]).reshape(Bsz, nc, ch, G, E, P)
    bmf = bm.astype(f32).reshape(Bsz, nc, ch, G, N)
    cmf = cm.astype(f32).reshape(Bsz, nc, ch, G, N)
    acum = jnp.cumsum((dt * a).reshape(Bsz, nc, ch, G, E), axis=2)
    seg = acum[:, :, :, None] - acum[:, :, None, :]
    causal = jnp.tril(jnp.ones((ch, ch), dtype=bool))[None, None, :, :, None, None]
    lmat = jnp.exp(jnp.where(causal, seg, -jnp.inf))
    cb = jnp.einsum('bclgn,bcsgn->bclsg', cmf, bmf)
    y_diag = jnp.einsum('bclsg,bclsge,bcsgep->bclgep', cb, lmat, xdt)
    decay_to_end = jnp.exp(acum[:, :, -1:] - acum)
    chunk_states = jnp.einsum('bclgn,bclge,bclgep->bcgepn', bmf, decay_to_end, xdt)
    chunk_decay = jnp.exp(acum[:, :, -1])

    def step(h, inp):
        dec, st = inp
        return h * dec[..., None, None] + st, h

    h_init = h0.astype(f32).reshape(Bsz, G, E, P, N)
    h_last, h_prev = lax.scan(step, h_init,
                              (jnp.moveaxis(chunk_decay, 1, 0), jnp.moveaxis(chunk_states, 1, 0)))
    y_off = jnp.einsum('bclgn,bclge,cbgepn->bclgep', cmf, jnp.exp(acum), h_prev)
    y = (y_diag + y_off).reshape(Bsz, L, H, P).astype(x.dtype)
    return y, h_last.reshape(Bsz, H, P, N)


def block(x, c, conv_buf, h0, lp):
    Bsz, L, _ = x.shape
    mod = jax.nn.silu(c) @ lp['w_ada'] + lp['b_ada']
    sh1, sc1, g1, sh2, sc2, g2 = jnp.split(mod[:, None, :], 6, axis=-1)
    h = x * (1 + sc1) + sh1
    proj = h @ lp['w_in']
    cuts = np.cumsum([D_CM, D_CM, D_INNER, CONV_DIM, SSD_HEADS, D_MODEL]).tolist()
    u, v, z, xbc, dt_raw, gate_cm, gate_ssd = jnp.split(proj, cuts, axis=-1)

    u = jax.nn.gelu(u, approximate=False)
    v = layer_norm(jax.nn.gelu(v, approximate=False), lp['ln_v_g'], lp['ln_v_b'])
    out_cm = u * chunk_spatial_mix(v, lp['w_spatial'], lp['b_spatial'])

    xbc_full = jnp.concatenate([conv_buf.astype(xbc.dtype), xbc], axis=1)
    new_conv = xbc_full[:, xbc_full.shape[1] - (CONV_K - 1):]
    xbc_c = jax.nn.silu(causal_dwconv(xbc_full, lp['conv_w'], lp['conv_b']))
    xs, bm, cmat = jnp.split(xbc_c, [D_INNER, D_INNER + SSD_GROUPS * SSD_STATE], axis=-1)
    xs = xs.reshape(Bsz, L, SSD_HEADS, SSD_HEAD_DIM)
    bm = bm.reshape(Bsz, L, SSD_GROUPS, SSD_STATE)
    cmat = cmat.reshape(Bsz, L, SSD_GROUPS, SSD_STATE)
    dt = jax.nn.softplus((dt_raw + lp['dt_bias']).astype(jnp.float32))
    a = -jnp.exp(lp['a_log'].astype(jnp.float32))
    y, h_last = ssd_scan(xs, dt, a, bm, cmat, h0)
    y = y + lp['d_skip'][:, None] * xs
    y = rms_norm(y.reshape(Bsz, L, D_INNER) * jax.nn.silu(z), lp['ssd_norm_w'])

    merged = (jax.nn.sigmoid(gate_cm + lp['b_gate'][:D_MODEL]) * (out_cm @ lp['w_cm_br'])
              + jax.nn.sigmoid(gate_ssd + lp['b_gate'][D_MODEL:]) * (y @ lp['w_ssd_br']))
    x = layer_norm(ALPHA * x + g1 * (merged @ lp['w_o']), lp['ln1_g'], lp['ln1_b'])

    h2 = x * (1 + sc2) + sh2
    ffn = (jax.nn.silu(h2 @ lp['w_ffn_gate']) * (h2 @ lp['w_ffn_up'])) @ lp['w_ffn_down']
    x = layer_norm(ALPHA * x + g2 * ffn, lp['ln2_g'], lp['ln2_b'])
    return x, new_conv, h_last.astype(h0.dtype), v


def setup_inputs(seed: int = 0) -> dict:
    key = jax.random.key(seed)
    ks = iter(jax.random.split(key, 40))
    f32 = jnp.float32

    def nrm(shape, s):
        return jax.random.normal(next(ks), shape, f32) * s

    dt0 = jnp.exp(jax.random.uniform(next(ks), (DEPTH, SSD_HEADS), f32,
                                     np.log(1e-3).astype(np.float32), np.log(1e-1).astype(np.float32)))
    return {
        'x_prompt': nrm((BATCH, SEQ, D_MODEL), 1.0),
        'x_sample': nrm((DEC_BATCH, DEC_SEQ, D_MODEL), 1.0),
        'state_ssd': nrm((DEPTH, DEC_BATCH, SSD_HEADS, SSD_HEAD_DIM, SSD_STATE), 0.1),
        'state_conv': nrm((DEPTH, DEC_BATCH, CONV_K - 1, CONV_DIM), 1.0),
        'c_prompt': nrm((BATCH, D_MODEL), 1.0),
        'c_sample': nrm((DEC_BATCH, D_MODEL), 1.0),
        'w_ada': nrm((DEPTH, D_MODEL, 6 * D_MODEL), 0.5 * D_MODEL ** -0.5),
        'b_ada': nrm((DEPTH, 6 * D_MODEL), 0.02),
        'w_in': nrm((DEPTH, D_MODEL, IN_COLS), D_MODEL ** -0.5),
        'b_gate': nrm((DEPTH, 2 * D_MODEL), 0.02),
        'ln_v_g': 1.0 + nrm((DEPTH, D_CM), 0.02),
        'ln_v_b': nrm((DEPTH, D_CM), 0.02),
        'w_spatial': nrm((DEPTH, CM_GROUPS, CHUNK, CHUNK), 0.5 * CHUNK ** -0.5),
        'b_spatial': 1.0 + nrm((DEPTH, CM_GROUPS, CHUNK), 0.02),
        'conv_w': nrm((DEPTH, CONV_K, CONV_DIM), CONV_K ** -0.5),
        'conv_b': nrm((DEPTH, CONV_DIM), 0.02),
        'dt_bias': dt0 + jnp.log(-jnp.expm1(-dt0)),
        'a_log': jnp.log(jax.random.uniform(next(ks), (DEPTH, SSD_HEADS), f32, 1.0, 16.0)),
        'd_skip': 1.0 + nrm((DEPTH, SSD_HEADS), 0.02),
        'ssd_norm_w': 1.0 + nrm((DEPTH, D_INNER), 0.02),
        'w_cm_br': nrm((DEPTH, D_CM, D_MODEL), D_CM ** -0.5),
        'w_ssd_br': nrm((DEPTH, D_INNER, D_MODEL), D_INNER ** -0.5),
        'w_o': nrm((DEPTH, D_MODEL, D_MODEL), BETA * D_MODEL ** -0.5),
        'ln1_g': 1.0 + nrm((DEPTH, D_MODEL), 0.02),
        'ln1_b': nrm((DEPTH, D_MODEL), 0.02),
        'w_ffn_gate': nrm((DEPTH, D_MODEL, D_FF), D_MODEL ** -0.5),
        'w_ffn_up': nrm((DEPTH, D_MODEL, D_FF), D_MODEL ** -0.5),
        'w_ffn_down': nrm((DEPTH, D_FF, D_MODEL), BETA * D_FF ** -0.5),
        'ln2_g': 1.0 + nrm((DEPTH, D_MODEL), 0.02),
        'ln2_b': nrm((DEPTH, D_MODEL), 0.02),
    }


def reference(x_prompt, x_sample, state_ssd, state_conv, c_prompt, c_sample,
              w_ada, b_ada, w_in, b_gate, ln_v_g, ln_v_b, w_spatial, b_spatial,
              conv_w, conv_b, dt_bias, a_log, d_skip, ssd_norm_w, w_cm_br, w_ssd_br,
              w_o, ln1_g, ln1_b, w_ffn_gate, w_ffn_up, w_ffn_down, ln2_g, ln2_b):
    def layer_params(i):
        return {'w_ada': w_ada[i], 'b_ada': b_ada[i], 'w_in': w_in[i], 'b_gate': b_gate[i],
                'ln_v_g': ln_v_g[i], 'ln_v_b': ln_v_b[i], 'w_spatial': w_spatial[i],
                'b_spatial': b_spatial[i], 'conv_w': conv_w[i], 'conv_b': conv_b[i],
                'dt_bias': dt_bias[i], 'a_log': a_log[i], 'd_skip': d_skip[i],
                'ssd_norm_w': ssd_norm_w[i], 'w_cm_br': w_cm_br[i], 'w_ssd_br': w_ssd_br[i],
                'w_o': w_o[i], 'ln1_g': ln1_g[i], 'ln1_b': ln1_b[i], 'w_ffn_gate': w_ffn_gate[i],
                'w_ffn_up': w_ffn_up[i], 'w_ffn_down': w_ffn_down[i], 'ln2_g': ln2_g[i],
                'ln2_b': ln2_b[i]}

    xp = x_prompt
    conv0_p = jnp.zeros((BATCH, CONV_K - 1, CONV_DIM), x_prompt.dtype)
    h0_p = jnp.zeros((BATCH, SSD_HEADS, SSD_HEAD_DIM, SSD_STATE), state_ssd.dtype)
    conv_p, ssd_p = [], []
    xs = x_sample
    conv_s, ssd_s, v_s = [], [], []
    for i in range(DEPTH):
        lp = layer_params(i)
        xp, cp, hp, _ = block(xp, c_prompt, conv0_p, h0_p, lp)
        conv_p.append(cp)
        ssd_p.append(hp)
        xs, cs, hs, vs = block(xs, c_sample, state_conv[i], state_ssd[i], lp)
        conv_s.append(cs)
        ssd_s.append(hs)
        v_s.append(vs)
    return (xp, xs, jnp.stack(ssd_p), jnp.stack(conv_p), jnp.stack(ssd_s), jnp.stack(conv_s), jnp.stack(v_s))
```

```python
import contextlib
import numpy as np
import concourse.bass as bass
import concourse.mybir as mybir
from concourse.bass_utils import run_bass_kernel_spmd

F32 = mybir.dt.float32
BF16 = mybir.dt.bfloat16
AF = mybir.ActivationFunctionType
ALU = mybir.AluOpType
AX = mybir.AxisListType

NCORES = 8
D = 1024
KC = 8
SEQ = 2048
T = 512
NCH = 4
NTILES = 4
NSMP = 16
W = T + NSMP
DI = 2048
NH = 32
HP = 64
NG = 4
NST = 128
CONV = 3072
DFF = 2816
KFF = 22
DEPTH = 2
ALPHA = 4.0 ** 0.25
EPS = 1e-5
EPS_LN = EPS / (ALPHA * ALPHA)
NSLAB = 46
SLAB = 4096
NSL = 4

BADA, BGATE, LNVG, LNVB, CONVW, CONVB, NORMW, LN1G, LN1B, LN2G, LN2B, WDIAG, BS0 = (
    0, 48, 64, 72, 80, 176, 200, 216, 224, 232, 240, 248, 256)
NPC = 264


class Op:
    __slots__ = ("eng", "fn", "deps", "signal", "count", "dsem", "dcount", "tag")

    def __init__(self, eng, fn):
        self.eng = eng
        self.fn = fn
        self.deps = []
        self.signal = False
        self.count = 0
        self.dsem = None
        self.dcount = 0


class Prog:
    ENGS = ("pe", "act", "dve", "pool", "sp")

    def __init__(self, nc):
        self.nc = nc
        self.eng_ops = {e: [] for e in self.ENGS}
        self.last_w = {}
        self.readers = {}
        self.dma_counts = {}
        self.nops = 0
        self.tag = ""
        self.names = None

    def add(self, eng, fn, reads=(), writes=(), dsem=None):
        op = Op(eng, fn)
        op.tag = self.tag
        seen = {}
        lw = self.last_w
        rd = self.readers
        for k in reads:
            w = lw.get(k)
            if w is not None:
                seen[id(w)] = (w, True)
        for k in writes:
            w = lw.get(k)
            if w is not None and id(w) not in seen:
                seen[id(w)] = (w, False)
            for r in rd.get(k, ()):
                if id(r) not in seen:
                    seen[id(r)] = (r, False)
        for d, raw in seen.values():
            if d.dsem is None and d.eng == eng:
                if eng == "pe":
                    continue
            op.deps.append(d)
            if d.dsem is None:
                d.signal = True
        if dsem is not None:
            op.dsem = dsem
            self.dma_counts[dsem] = self.dma_counts.get(dsem, 0) + 16
            op.dcount = self.dma_counts[dsem]
        for k in reads:
            rd.setdefault(k, []).append(op)
        for k in writes:
            lw[k] = op
            rd[k] = []
        self.eng_ops[eng].append(op)
        self.nops += 1
        return op

    def emit(self, sems, dsems):
        nc = self.nc
        for e in self.ENGS:
            c = 0
            for op in self.eng_ops[e]:
                if op.dsem is None and op.signal:
                    c += 1
                    op.count = c
        with nc.Block() as block:
            def run(e):
                def body(eng):
                    waited = {}
                    for op in self.eng_ops[e]:
                        need = {}
                        for d in op.deps:
                            if d.dsem is not None:
                                key = ("d", d.dsem)
                                val = d.dcount
                            else:
                                key = ("e", d.eng)
                                val = d.count
                            if val > need.get(key, 0):
                                need[key] = val
                        for key, val in need.items():
                            if waited.get(key, 0) >= val:
                                continue
                            waited[key] = val
                            s = dsems[key[1]] if key[0] == "d" else sems[key[1]]
                            eng.wait_ge(s, val)
                        ins = op.fn(eng)
                        if self.names is not None:
                            try:
                                self.names[ins.ins.name] = op.tag
                            except Exception:
                                pass
                        if op.dsem is not None:
                            ins.then_inc(dsems[op.dsem], 16)
                        elif op.signal:
                            ins.then_inc(sems[e], 1)
                    if e == "sp":
                        for name, cnt in self.dma_counts.items():
                            eng.wait_ge(dsems[name], cnt)
                return body
            block.tensor(run("pe"))
            block.scalar(run("act"))
            block.vector(run("dve"))
            block.gpsimd(run("pool"))
            block.sync(run("sp"))


class V:
    __slots__ = ("ap", "keys")

    def __init__(self, ap, keys):
        self.ap = ap
        self.keys = keys


KEYG = 512


class Buf:
    def __init__(self, arena, off, shape, dtype):
        self.off = off
        self.shape = tuple(shape)
        self.dtype = dtype
        self.es = 4 if dtype == F32 else 2
        n = 1
        for s in shape:
            n *= s
        self.n = n
        self.nbytes = n * self.es
        assert off % 4 == 0
        ap = arena[:, off // 2: off // 2 + (self.nbytes + 1) // 2]
        if dtype == F32:
            ap = ap.bitcast(F32)
        if len(shape) == 2:
            ap = ap.rearrange("p (a b) -> p a b", a=shape[0])
        elif len(shape) == 3:
            ap = ap.rearrange("p (a b c) -> p a b c", a=shape[0], b=shape[1])
        self.ap = ap

    def _keys(self, lo, hi):
        b0 = (self.off + lo * self.es) // KEYG
        b1 = (self.off + hi * self.es - 1) // KEYG
        return range(b0, b1 + 1)

    def v(self, *idx, part=None):
        shape = self.shape
        assert len(idx) == len(shape)
        sl = []
        rngs = []
        for i, s in zip(idx, shape):
            if i is None:
                sl.append(slice(None))
                rngs.append((0, s))
            elif isinstance(i, tuple):
                sl.append(slice(i[0], i[1]))
                rngs.append(i)
            else:
                sl.append(i)
                rngs.append((i, i + 1))
        psl = slice(None) if part is None else slice(part[0], part[1])
        ap = self.ap[(psl,) + tuple(sl)]
        keys = set()
        if len(shape) == 1:
            keys.update(self._keys(rngs[0][0], rngs[0][1]))
        elif len(shape) == 2:
            L = shape[1]
            if rngs[1] == (0, L):
                keys.update(self._keys(rngs[0][0] * L, rngs[0][1] * L))
            else:
                for a in range(*rngs[0]):
                    keys.update(self._keys(a * L + rngs[1][0], a * L + rngs[1][1]))
        else:
            L1, L2 = shape[1], shape[2]
            for a in range(*rngs[0]):
                if rngs[2] == (0, L2):
                    keys.update(self._keys((a * L1 + rngs[1][0]) * L2, (a * L1 + rngs[1][1]) * L2))
                else:
                    for b in range(*rngs[1]):
                        base = (a * L1 + b) * L2
                        keys.update(self._keys(base + rngs[2][0], base + rngs[2][1]))
        return V(ap, list(keys))

    def all(self):
        return self.v(*([None] * len(self.shape)))


def _slab_pack(Wm, c0, cw, k0, nk):
    blk = Wm[k0 * 128:(k0 + nk) * 128, c0:c0 + cw]
    cwv = blk.shape[1]
    out = np.zeros((128, SLAB), np.float32)
    tmp = blk.reshape(nk, 128, cwv).transpose(1, 0, 2)
    o3 = out[:, :nk * cw].reshape(128, nk, cw)
    o3[:, :, :cwv] = tmp
    return out


def _pcols(vec):
    return np.ascontiguousarray(vec.reshape(-1, 128).T)


def build_program(n_tiles=NTILES, with_samples=True):
    nc = bass.Bass("TRN2", target_bir_lowering=False)

    def din(name, shape):
        return nc.dram_tensor(name, list(shape), F32, kind="ExternalInput").ap()

    def dout(name, shape):
        return nc.dram_tensor(name, list(shape), F32, kind="ExternalOutput").ap()

    xp_d = din("xp", [SEQ, D])
    xs_d = din("xs", [NSMP, D])
    cc_d = din("cc", [NSMP + 1, D])
    sssd_d = din("sssd", [DEPTH, NSMP * 8, 4, HP * NST])
    sconv_d = din("sconv", [DEPTH, NSMP * 3, CONV])
    wada_d = din("wada", [DEPTH, 12, 128, SLAB])
    wstr_d = din("wstr", [DEPTH, NSLAB, 128, SLAB])
    wdt_d = din("wdt", [128, DEPTH * KC * 32])
    pcol_d = din("pcol", [128, DEPTH * NPC])
    prow_d = din("prow", [128, DEPTH * 96])
    pq_d = din("pq", [128, DEPTH * 8])
    wst_d = din("wst", [128, DEPTH * 8 * 128])
    bsb_d = din("bsb", [128, DEPTH * 1024])
    lnvbc_d = din("lnvbc", [NSMP, DEPTH * 2 * 1024])
    consts_d = din("consts", [128, 6 * 128])

    yp_d = dout("yp", [SEQ, D])
    ys_d = dout("ys", [NSMP, D])
    ssdp_d = dout("ssdp", [DEPTH, DI, NST])
    convp_d = dout("convp", [DEPTH, 3, CONV])
    ssds_d = dout("ssds", [DEPTH, NSMP * 8, 4, HP * NST])
    convs_d = dout("convs", [DEPTH, NSMP, 3, CONV])
    vs_d = dout("vs", [DEPTH, NSMP, D])

    wcache = nc.dram_tensor("wcache", [DEPTH, NSLAB, 128, SLAB], BF16).ap()
    scrST = nc.dram_tensor("scrST", [DEPTH, 128, DI], F32).ap()
    scrX = nc.dram_tensor("scrX", [DEPTH, NSMP, DI], F32).ap()
    scrZ = nc.dram_tensor("scrZ", [DEPTH, NSMP, DI], F32).ap()
    scrB = nc.dram_tensor("scrB", [DEPTH, NSMP, 1024], F32).ap()
    scrC = nc.dram_tensor("scrC", [DEPTH, NSMP, 1024], F32).ap()
    scrD = nc.dram_tensor("scrD", [DEPTH, NSMP, 32], F32).ap()

    p = Prog(nc)
    st = contextlib.ExitStack()
    E = st.enter_context

    P_BYTES = 90 * 1024
    DYN_BYTES = 98 * 1024
    SMP_BYTES = 18 * 1024
    ARENA = P_BYTES + DYN_BYTES + SMP_BYTES
    arena_t = E(nc.sbuf_tensor("arena", [128, ARENA // 2], BF16))
    arena = arena_t[:]
    psum_t = E(nc.psum_tensor("psum", [128, 8, 512], F32))
    PSA = psum_t[:]

    class Bump:
        def __init__(self, base, limit):
            self.o = base
            self.limit = limit

        def __call__(self, shape, dtype, align=512):
            self.o = (self.o + align - 1) // align * align
            b = Buf(arena, self.o, shape, dtype)
            self.o += b.nbytes
            assert self.o <= self.limit, (self.o, self.limit)
            return b

    pa = Bump(0, P_BYTES)
    xT = pa([KC, W], F32)
    hT = pa([KC, W], BF16)
    mod = pa([DEPTH, 48, 17], F32)
    mod2 = pa([DEPTH, 16, 17], F32)
    pcol = pa([DEPTH, NPC], F32)
    prow = pa([DEPTH, 96], F32)
    abc = pa([DEPTH, 32], F32)
    cf32 = pa([6, 128], F32)
    cb16 = pa([4, 128], BF16)
    cM = pa([128], BF16)
    wsTb = pa([DEPTH, 8, 128], BF16)
    BIASb = pa([DEPTH, 8, 128], F32)
    wdt = pa([DEPTH, KC, 32], BF16)
    hist = pa([DEPTH, 24, 3], F32)
    csT = pa([KC, 17], BF16)
    smalls = pa([64], F32)
    slabs = [pa([SLAB], BF16) for _ in range(NSL)]
    P_USED = pa.o

    DB = P_BYTES
    DL = P_BYTES + DYN_BYTES

    def dyn(off):
        return Bump(DB + off, DL)

    a = dyn(0)
    zs = a([NCH, DI], BF16)
    xs_tm = a([NCH, DI], BF16)
    bmT = a([NG, T], BF16)
    cmT = a([NG, T], BF16)
    bm_tm = a([NCH, 512], BF16)
    dtall = a([NCH + 1, 32], F32)
    daall = a([NCH, 32], F32)
    yznT = a([16, T], BF16)
    A_TMP = a.o - DB
    a1 = dyn(A_TMP)
    NRB = 4
    raw = [a1([515], F32) for _ in range(NRB)]
    acc = [a1([T], F32) for _ in range(NRB)]
    xsTk = [a1([T], BF16) for _ in range(NRB)]
    xstage = [a1([D], F32) for _ in range(2)]
    a2 = dyn(A_TMP)
    STf = a2([DI], F32)
    STb = a2([DI], BF16)
    yzb = [a2([DI], BF16) for _ in range(2)]
    Rg = [a2([4, 128], BF16) for _ in range(3)]
    Lx = [a2([4, 128], BF16) for _ in range(2)]
    cbm = [a2([NG, 128], BF16) for _ in range(2)]
    t1 = [a2([256], F32) for _ in range(3)]
    tD = [a2([256], F32) for _ in range(1)]
    xdtd = [a2([512], BF16) for _ in range(1)]
    Eexp = [a2([96], F32) for _ in range(2)]
    ssq = [a2([16], F32, align=64) for _ in range(2)]
    w2 = [a2([32], F32, align=64) for _ in range(2)]
    lndt = a2([NCH, 32], F32)
    Dg = a2([128], BF16)
    junkb = a2([512], BF16)
    b_ = dyn(0)
    uT = b_([KC, W], BF16)
    sg = b_([2, KC, W], BF16)
    mg = b_([KC, W], BF16)
    vln = b_([NCH, D], BF16)
    tmix = b_([8, 128], F32)
    assert b_.o - DB <= (yznT.off - DB), (b_.o - DB, yznT.off - DB)
    b2 = dyn(A_TMP)
    gv = [b2([D], F32) for _ in range(2)]
    junk2 = b2([D], BF16)
    tmpf = [b2([T], F32) for _ in range(2)]
    vstat = b2([16], F32)
    c_ = dyn(0)
    actb = c_([KFF, W], BF16)
    sgt = [c_([W], F32) for _ in range(2)]
    xb = [c_([W], BF16) for _ in range(2)]
    sqb = [c_([W], BF16) for _ in range(2)]
    msb = c_([W], F32)
    v1 = c_([W], F32)
    rstd = c_([W], F32)
    nmr = c_([W], F32)
    xn = [c_([W], F32) for _ in range(2)]
    ystage = [c_([D], F32) for _ in range(2)]
    pr = dyn(0)
    wst_f = pr([DEPTH, 8, 128], F32)
    bsb_f = pr([DEPTH, 1024], F32)
    cc_f = pr([D], F32)
    pq = pr([DEPTH, 8], F32)
    lnvg_s = pr([D], F32)
    lnvb_s = pr([D], F32)

    SBASE = DL
    PP = 4

    def smpa(off):
        return Bump(SBASE + off, ARENA)
    sp_ = smpa(0)
    aq = sp_([DEPTH, 4], F32, align=64)
    dq = sp_([DEPTH, 4], F32, align=64)
    vlnTs = sp_([KC, NSMP], F32, align=64)
    yznTs = sp_([16, NSMP], BF16, align=64)
    stmp = sp_([KC, NSMP], F32, align=64)
    stmp2 = sp_([KC, NSMP], F32, align=64)
    sgS = sp_([2, KC, NSMP], BF16, align=64)
    ocmS = sp_([KC, NSMP], BF16, align=64)
    xbS = [sp_([NSMP], BF16, align=32) for _ in range(2)]
    sqbS = [sp_([NSMP], BF16, align=32) for _ in range(2)]
    lnS = sp_([4, NSMP], F32, align=64)
    S_DYN = (sp_.o - SBASE + 511) // 512 * 512
    s1 = smpa(S_DYN)
    rawS = s1([24, NSMP], F32)
    oldT = s1([24, 48], F32)
    xbcS = s1([24, NSMP], F32)
    accS = s1([24, NSMP], F32)
    tmpS = s1([24, NSMP], F32)
    s2 = smpa(S_DYN)
    xs_q = s2([256], F32)
    B_q = s2([128], F32)
    C_q = s2([128], F32)
    dtq = s2([16], F32)
    xdt_q = s2([256], F32)
    y_q = s2([256], F32)
    bufA = [s2([PP, 128], F32) for _ in range(3)]
    bufB = [s2([PP, 128], F32) for _ in range(2)]
    zs_q = Buf(arena, bufB[1].off, [256], F32)
    print("SBUF regions: P", P_USED, "/", P_BYTES, " A", a2.o - DB, "B", b_.o - DB, b2.o - DB, "C", c_.o - DB,
          "/", DYN_BYTES, " S", s1.o - SBASE, s2.o - SBASE, "/", SMP_BYTES)

    sems = {e: E(nc.semaphore("s_" + e)) for e in Prog.ENGS}
    dsem_names = ([f"sl{i}" for i in range(NSL)] + [f"wb{i}" for i in range(NSL)] + ["xs0", "xs1", "ys0", "ys1", "stl", "sts0", "sts1", "fin"]
                  + [f"i{i}" for i in range(12)]
                  + ["sq0", "sq1", "sq2", "sq3", "sq4", "sq5", "sa0", "sa1", "sa2", "so0", "so1", "so2", "sg0", "sg1", "sg2", "ss0", "ss1", "scv", "sd2d", "svs"])
    dsems = {n: E(nc.semaphore("d_" + n)) for n in dsem_names}

    def ks(*vs):
        out = []
        for v_ in vs:
            if v_ is not None:
                out.extend(v_.keys)
        return out

    def PS(bank, lo=0, hi=512, part=None):
        psl = slice(None) if part is None else slice(part[0], part[1])
        return V(PSA[psl, bank, lo:hi], [("ps", bank)])

    def PS2(b0, nb):
        return V(PSA[:, b0:b0 + nb, :], [("ps", b0 + i) for i in range(nb)])

    def MM(out, lhsT, rhs, start=True, stop=True, extra_r=()):
        p.add("pe", lambda e, o=out.ap, l=lhsT.ap, r=rhs.ap, s=start, t=stop:
              e.matmul(o, lhsT=l, rhs=r, start=s, stop=t, skip_group_check=True),
              reads=ks(lhsT, rhs) + list(extra_r), writes=out.keys)

    def TR(out, in_, ident):
        p.add("pe", lambda e, o=out.ap, i=in_.ap, d=ident.ap: e.transpose(o, i, d),
              reads=ks(in_, ident), writes=out.keys)

    def psum_rw(v_):
        return isinstance(v_.keys[0], tuple) and v_.keys[0][0] == "ps"

    def _rw(out, ins):
        reads = []
        writes = list(out.keys)
        for v_ in ins:
            if v_ is None:
                continue
            if psum_rw(v_):
                writes.extend(v_.keys)
            else:
                reads.extend(v_.keys)
        return reads, writes

    def ACT(out, in_, func, bias=None, scale=None, accum=None):
        reads, writes = _rw(out, [in_, bias if isinstance(bias, V) else None,
                                  scale if isinstance(scale, V) else None])
        if accum is not None:
            writes = writes + accum.keys
        kw = {}
        if bias is not None:
            kw["bias"] = bias.ap if isinstance(bias, V) else bias
        if scale is not None:
            kw["scale"] = scale.ap if isinstance(scale, V) else scale
        if accum is not None:
            kw["accum_out"] = accum.ap
        p.add("act", lambda e, o=out.ap, i=in_.ap, f=func, kw=kw: e.activation(out=o, in_=i, func=f, **kw),
              reads=reads, writes=writes)

    def sc(x):
        return x.ap if isinstance(x, V) else x

    def TS(eng, out, in0, s1, s2, op0, op1=None):
        reads, writes = _rw(out, [in0, s1 if isinstance(s1, V) else None, s2 if isinstance(s2, V) else None])
        if op1 is None:
            p.add(eng, lambda e, o=out.ap, i=in0.ap, a=sc(s1), f=op0:
                  e.tensor_scalar(out=o, in0=i, scalar1=a, scalar2=None, op0=f), reads=reads, writes=writes)
        else:
            p.add(eng, lambda e, o=out.ap, i=in0.ap, a=sc(s1), b=sc(s2), f=op0, g=op1:
                  e.tensor_scalar(out=o, in0=i, scalar1=a, scalar2=b, op0=f, op1=g), reads=reads, writes=writes)

    def TT(eng, out, in0, in1, op, in1_ap=None, in0_ap=None, out_ap=None):
        reads, writes = _rw(out, [in0, in1])
        p.add(eng, lambda e, o=(out_ap if out_ap is not None else out.ap),
              i=(in0_ap if in0_ap is not None else in0.ap),
              j=(in1_ap if in1_ap is not None else in1.ap), f=op:
              e.tensor_tensor(out=o, in0=i, in1=j, op=f), reads=reads, writes=writes)

    def STT(eng, out, in0, s, in1, op0, op1, in1_ap=None, in0_ap=None):
        reads, writes = _rw(out, [in0, in1, s if isinstance(s, V) else None])
        p.add(eng, lambda e, o=out.ap, i=(in0_ap if in0_ap is not None else in0.ap), a=sc(s),
              j=(in1_ap if in1_ap is not None else in1.ap), f=op0, g=op1:
              e.scalar_tensor_tensor(out=o, in0=i, scalar=a, in1=j, op0=f, op1=g), reads=reads, writes=writes)

    def CP(eng, out, in_, out_ap=None, in_ap=None):
        reads, writes = _rw(out, [in_])
        o = out_ap if out_ap is not None else out.ap
        i = in_ap if in_ap is not None else in_.ap
        if eng == "act":
            p.add("act", lambda e, o=o, i=i: e.copy(out=o, in_=i), reads=reads, writes=writes)
        else:
            p.add(eng, lambda e, o=o, i=i: e.tensor_copy(out=o, in_=i), reads=reads, writes=writes)

    def MEMSET(eng, out, val):
        p.add(eng, lambda e, o=out.ap, v_=val: e.memset(o, v_), writes=out.keys)

    def DMA(q, out_ap, in_ap, sem, reads=(), writes=()):
        p.add(q, lambda e, o=out_ap, i=in_ap: e.dma_start(out=o, in_=i), reads=list(reads), writes=list(writes),
              dsem=sem)

    def bc_last(v_, n):
        sh = list(v_.ap.shape)
        return v_.ap.unsqueeze(len(sh)).to_broadcast(sh + [n])

    def bc_mid(v_, n):
        sh = list(v_.ap.shape)
        return v_.ap.unsqueeze(1).to_broadcast([sh[0], n] + sh[1:])

    stream = []
    for s in range(12):
        stream.append((wada_d[0, s], None, None))
    for t in range(n_tiles):
        for l in range(DEPTH):
            if t == 0 and l == 1:
                for s in range(12):
                    stream.append((wada_d[1, s], None, None))
            for s in range(NSLAB):
                if t == 0:
                    stream.append((wstr_d[l, s], None, (l, s)) if n_tiles > 1 else (wstr_d[l, s], None, None))
                else:
                    stream.append((wcache[l, s], ("wc", l, s), None))
    wstate = {"loaded": 0, "cur": 0}

    def slab_issue():
        i = wstate["loaded"]
        if i >= len(stream):
            return
        slot = i % NSL
        src, ckey, _ = stream[i]
        DMA("pool", slabs[slot].ap, src, f"sl{slot}", reads=([ckey] if ckey else []), writes=slabs[slot].all().keys)
        wstate["loaded"] += 1

    def slab_next():
        i = wstate["cur"]
        assert i < wstate["loaded"]
        wstate["cur"] += 1
        slot = i % NSL
        wb = stream[i][2]
        if wb is not None:
            DMA("sp", wcache[wb[0], wb[1]], slabs[slot].ap, f"wb{slot}", reads=slabs[slot].all().keys,
                writes=[("wc", wb[0], wb[1])])
        return slabs[slot]

    def slab_done():
        slab_issue()

    def slab_v(sb, nk, cw, k, c0, c1):
        lo = k * cw + c0
        return V(sb.ap[:, lo:lo + (c1 - c0)], list(sb._keys(lo, lo + (c1 - c0))))

    bankctr = {"n": 0}

    def nextbank():
        b = bankctr["n"] % 4
        bankctr["n"] += 1
        return b

    for _ in range(NSL):
        slab_issue()
    DMA("sp", cf32.ap, consts_d.rearrange("p (a b) -> p a b", a=6), "i0", writes=cf32.all().keys)
    DMA("sp", pcol.ap, pcol_d.rearrange("p (a b) -> p a b", a=DEPTH), "i1", writes=pcol.all().keys)
    DMA("sp", prow.ap, prow_d.rearrange("p (a b) -> p a b", a=DEPTH), "i2", writes=prow.all().keys)
    DMA("sp", cc_f.ap[0:17, :], cc_d, "i3", writes=cc_f.all().keys)
    DMA("sp", wst_f.ap, wst_d.rearrange("p (a b c) -> p a b c", a=DEPTH, b=8), "i4", writes=wst_f.all().keys)
    DMA("sp", bsb_f.ap, bsb_d.rearrange("p (a b) -> p a b", a=DEPTH), "i5", writes=bsb_f.all().keys)
    DMA("pool", wdt.ap, wdt_d.rearrange("p (a b c) -> p a b c", a=DEPTH, b=KC), "i6", writes=wdt.all().keys)
    DMA("sp", pq.ap, pq_d.rearrange("p (a b) -> p a b", a=DEPTH), "i7", writes=pq.all().keys)

    ident = cf32.v(0, None)
    triu = cf32.v(1, None)
    trilS = cf32.v(2, None)
    ones = cf32.v(3, None)
    CP("dve", cb16.all(), cf32.v((0, 4), None))
    identb = cb16.v(0, None)
    triub = cb16.v(1, None)
    trilSb = cb16.v(2, None)
    MEMSET("dve", cM.all(), 1.0 / 1024.0)
    MEMSET("dve", hist.all(), 0.0)
    ACT(abc.all(), prow.v(None, (32, 64)), AF.Exp)
    TS("dve", abc.all(), abc.all(), -1.0, None, ALU.mult)
    ACT(aq.all(), pq.v(None, (0, 4)), AF.Exp)
    TS("dve", aq.all(), aq.all(), -1.0, None, ALU.mult)
    CP("dve", dq.all(), pq.v(None, (4, 8)))
    TT("dve", wst_f.all(), wst_f.all(), triu, ALU.mult,
       in1_ap=triu.ap.unsqueeze(1).unsqueeze(1).to_broadcast([128, DEPTH, 8, 128]))
    CP("act", wsTb.all(), wst_f.all())
    for l in range(DEPTH):
        for hf in range(2):
            MM(PS(hf), ones, V(wst_f.ap[:, l, :, :].rearrange("p a b -> p (a b)")[:, hf * 512:hf * 512 + 512], wst_f.v(l, (4 * hf, 4 * hf + 4), None).keys))
        for g in range(8):
            hf, gg = divmod(g, 4)
            STT("dve", BIASb.v(l, g, None), PS(hf, gg * 128, gg * 128 + 128), pcol.v(l, (LNVB + g, LNVB + g + 1)),
                bsb_f.v(l, (g * 128, g * 128 + 128)), ALU.mult, ALU.add)
    ACT(V(cc_f.ap[0:17, :], cc_f.all().keys), V(cc_f.ap[0:17, :], cc_f.all().keys), AF.Silu)
    for k in range(KC):
        TR(PS(4, k * 17, k * 17 + 17), V(cc_f.ap[0:17, k * 128:(k + 1) * 128], cc_f.all().keys),
           V(cf32.ap[0:17, 0, 0:17], ident.keys))
    CP("dve", csT.all(), PS(4, 0, KC * 17), in_ap=PSA[:, 4, 0:KC * 17].rearrange("p (a b) -> p a b", a=KC))
    def compute_mod(l):
        for s in range(12):
            sb = slab_next()
            for jj in range(4):
                j = 4 * s + jj
                b = nextbank()
                for k in range(KC):
                    MM(PS(b, 0, 17), slab_v(sb, KC, 512, k, jj * 128, jj * 128 + 128), csT.v(k, None),
                       start=(k == 0), stop=(k == KC - 1))
                TS("dve", mod.v(l, j, None), PS(b, 0, 17), pcol.v(l, (BADA + j, BADA + j + 1)), None, ALU.add)
            slab_done()
        TS("dve", mod.v(l, (8, 16), None), mod.v(l, (8, 16), None), 1.0, None, ALU.add)
        TS("dve", mod.v(l, (32, 40), None), mod.v(l, (32, 40), None), 1.0, None, ALU.add)
        TS("dve", mod.v(l, (16, 24), None), mod.v(l, (16, 24), None), 1.0 / ALPHA, None, ALU.mult)
        TS("dve", mod.v(l, (40, 48), None), mod.v(l, (40, 48), None), 1.0 / ALPHA, None, ALU.mult)
        TT("dve", mod2.v(l, (0, 8), None), mod.v(l, (32, 40), None), pcol.v(l, (LN1G, LN1G + 8)), ALU.mult,
           in1_ap=bc_last(pcol.v(l, (LN1G, LN1G + 8)), 17))
        TT("dve", mod2.v(l, (8, 16), None), mod.v(l, (32, 40), None), pcol.v(l, (LN1B, LN1B + 8)), ALU.mult,
           in1_ap=bc_last(pcol.v(l, (LN1B, LN1B + 8)), 17))
        TT("dve", mod2.v(l, (8, 16), None), mod2.v(l, (8, 16), None), mod.v(l, (24, 32), None), ALU.add)
    compute_mod(0)
    MEMSET("dve", STf.all(), 0.0)
    for l in range(DEPTH):
        DMA("sp", scrST[l], STf.ap, f"sts{l}", reads=STf.all().keys, writes=[("scrST", l)])

    def modc(l, j, col=0):
        return mod.v(l, j, (col, col + 1))

    def layer_norm_apply(l, which, smp, part="both"):
        gcol = LN1G if which == 1 else LN2G
        bcol = LN1B if which == 1 else LN2B
        if part == "smp":
            return _ln_smp(l, which, gcol, bcol)
        _ln_main(l, which, gcol, bcol)
        if smp and part == "both":
            _ln_smp(l, which, gcol, bcol)

    def _ln_main(l, which, gcol, bcol):
        for j in range(KC):
            q = j % 2
            ACT(xb[q].v((0, T)), xT.v(j, (0, T)), AF.Copy)
            ACT(sqb[q].v((0, T)), xT.v(j, (0, T)), AF.Square)
            MM(PS(6), cMm, xb[q].v((0, T)), start=(j == 0), stop=(j == KC - 1))
            MM(PS(7), cMm, sqb[q].v((0, T)), start=(j == 0), stop=(j == KC - 1))
        CP("act", msb.v((0, T)), PS(6))
        TT("dve", v1.v((0, T)), msb.v((0, T)), msb.v((0, T)), ALU.mult)
        TT("dve", v1.v((0, T)), PS(7), v1.v((0, T)), ALU.subtract)
        ACT(v1.v((0, T)), v1.v((0, T)), AF.Sqrt, bias=epsln_c, scale=1.0)
        p.add("dve", lambda e, o=rstd.ap[:, 0:T], i=v1.ap[:, 0:T]: e.reciprocal(out=o, in_=i),
              reads=v1.v((0, T)).keys, writes=rstd.v((0, T)).keys)
        STT("dve", nmr.v((0, T)), msb.v((0, T)), -1.0, rstd.v((0, T)), ALU.mult, ALU.mult)
        for j in range(KC):
            q = j % 2
            TT("dve", xn[q].v((0, T)), xT.v(j, (0, T)), rstd.v((0, T)), ALU.mult)
            TT("pool", xn[q].v((0, T)), xn[q].v((0, T)), nmr.v((0, T)), ALU.add)
            ACT(xT.v(j, (0, T)), xn[q].v((0, T)), AF.Identity, bias=pcol.v(l, (bcol + j, bcol + j + 1)),
                scale=pcol.v(l, (gcol + j, gcol + j + 1)))
            if which == 1:
                TS("dve", hT.v(j, (0, T)), xn[q].v((0, T)), mod2.v(l, j, (0, 1)), mod2.v(l, 8 + j, (0, 1)),
                   ALU.mult, ALU.add)

    def _ln_smp(l, which, gcol, bcol):
        if True:
            for j in range(KC):
                q = j % 2
                ACT(xbS[q].all(), xT.v(j, SC), AF.Copy)
                ACT(sqbS[q].all(), xT.v(j, SC), AF.Square)
                MM(PS(4, 0, NSMP), cMm, xbS[q].all(), start=(j == 0), stop=(j == KC - 1))
                MM(PS(5, 0, NSMP), cMm, sqbS[q].all(), start=(j == 0), stop=(j == KC - 1))
            CP("act", lnS.v(0, None), PS(4, 0, NSMP))
            TT("dve", lnS.v(1, None), lnS.v(0, None), lnS.v(0, None), ALU.mult)
            TT("dve", lnS.v(1, None), PS(5, 0, NSMP), lnS.v(1, None), ALU.subtract)
            ACT(lnS.v(1, None), lnS.v(1, None), AF.Sqrt, bias=epsln_c, scale=1.0)
            p.add("dve", lambda e, o=lnS.ap[:, 2, :], i=lnS.ap[:, 1, :]: e.reciprocal(out=o, in_=i),
                  reads=lnS.v(1, None).keys, writes=lnS.v(2, None).keys)
            STT("dve", lnS.v(3, None), lnS.v(0, None), -1.0, lnS.v(2, None), ALU.mult, ALU.mult)
            TT("dve", stmp2.all(), xT.v(None, SC), lnS.v(2, None), ALU.mult, in1_ap=bc_mid(lnS.v(2, None), KC))
            TT("dve", stmp2.all(), stmp2.all(), lnS.v(3, None), ALU.add, in1_ap=bc_mid(lnS.v(3, None), KC))
            TT("dve", stmp.all(), stmp2.all(), pcol.v(l, (gcol, gcol + 8)), ALU.mult,
               in1_ap=bc_last(pcol.v(l, (gcol, gcol + 8)), NSMP))
            TT("dve", xT.v(None, SC), stmp.all(), pcol.v(l, (bcol, bcol + 8)), ALU.add,
               in1_ap=bc_last(pcol.v(l, (bcol, bcol + 8)), NSMP))
            if which == 1:
                TT("dve", stmp.all(), stmp2.all(), mod2.v(l, (0, 8), (1, 17)), ALU.mult)
                TT("dve", hT.v(None, SC), stmp.all(), mod2.v(l, (8, 16), (1, 17)), ALU.add)

    cMm = cM.all()
    epsln_c = smalls.v((0, 1))
    eps_c = smalls.v((1, 2))
    MEMSET("dve", smalls.v((0, 1)), EPS_LN)
    MEMSET("dve", smalls.v((1, 2)), EPS)

    sbctr = {"n": 0}

    def sbank():
        b = 4 + (sbctr["n"] % 2)
        sbctr["n"] += 1
        return b

    SC = (T, W)

    def stage16(q):
        return V(xstage[q].ap[0:16, 0:512], xstage[q].v((0, 512)).keys)

    def smp_ride_fm(sb, nk, cw, c0, rhs_fn, evac):
        b = sbank()
        for k in range(nk):
            MM(PS(b, 0, NSMP), slab_v(sb, nk, cw, k, c0, c0 + 128), rhs_fn(k), start=(k == 0), stop=(k == nk - 1))
        evac(PS(b, 0, NSMP))

    def smp_ride_tm(sb, evac):
        b = sbank()
        for k in range(KC):
            MM(PS(b, 0, 512, part=(0, NSMP)), hT.v(k, SC), slab_v(sb, KC, 512, k, 0, 512),
               start=(k == 0), stop=(k == KC - 1))
        evac(PS(b, 0, 512, part=(0, NSMP)))

    def smp_conv(l):
        for pc in range(6):
            q = pc % 2
            st_v = V(xstage[q].ap[0:48, 0:512], xstage[q].v((0, 512)).keys)
            DMA("sp", st_v.ap, sconv_d[l][:, pc * 512:(pc + 1) * 512], f"xs{q}", writes=st_v.keys)
            b = sbank()
            for kk in range(4):
                TR(PS(b, kk * 48, kk * 48 + 48), V(xstage[q].ap[0:48, kk * 128:(kk + 1) * 128], st_v.keys),
                   V(cf32.ap[0:48, 0, 0:48], ident.keys))
            CP("dve", oldT.v((4 * pc, 4 * pc + 4), None), PS(b, 0, 192),
               in_ap=PSA[:, b, 0:192].rearrange("p (a b) -> p a b", a=4))
        DMA("sp", convs_d[l][:, 0:2, :], sconv_d[l].rearrange("(b k) c -> b k c", k=3)[:, 1:3, :], "sd2d")
        old4 = oldT.ap.rearrange("p j (b k) -> p j b k", k=3)
        wk = lambda kk: bc_last(pcol.v(l, (CONVW + kk * 24, CONVW + kk * 24 + 24)), NSMP)
        TT("dve", accS.all(), oldT.all(), pcol.v(l, (CONVW, CONVW + 24)), ALU.mult, in0_ap=old4[:, :, :, 0], in1_ap=wk(0))
        for kk in (1, 2):
            TT("dve", tmpS.all(), oldT.all(), pcol.v(l, (CONVW + kk * 24, CONVW + kk * 24 + 24)), ALU.mult,
               in0_ap=old4[:, :, :, kk], in1_ap=wk(kk))
            TT("dve", accS.all(), accS.all(), tmpS.all(), ALU.add)
        TT("dve", tmpS.all(), rawS.all(), pcol.v(l, (CONVW + 72, CONVW + 96)), ALU.mult, in1_ap=wk(3))
        TT("dve", accS.all(), accS.all(), tmpS.all(), ALU.add)
        TT("dve", accS.all(), accS.all(), pcol.v(l, (CONVB, CONVB + 24)), ALU.add,
           in1_ap=bc_last(pcol.v(l, (CONVB, CONVB + 24)), NSMP))
        ACT(xbcS.all(), accS.all(), AF.Silu)
        for pc in range(6):
            q = pc % 2
            b = sbank()
            for kk in range(4):
                j = 4 * pc + kk
                TR(PS(b, kk * 128, kk * 128 + 128, part=(0, NSMP)), xbcS.v(j, None), ident)
            CP("act", stage16(q), PS(b, 0, 512, part=(0, NSMP)))
            if pc < 4:
                DMA("sp", scrX[l][:, pc * 512:(pc + 1) * 512], stage16(q).ap, f"ss{q}", reads=stage16(q).keys,
                    writes=[("scrX", l)])
            else:
                dst = scrB if pc == 4 else scrC
                for dup in range(2):
                    DMA("sp", dst[l].rearrange("b (g d n) -> b g d n", g=NG, d=2)[:, :, dup, :],
                        stage16(q).ap.rearrange("b (g n) -> b g n", g=NG), f"ss{q}", reads=stage16(q).keys,
                        writes=[("scrBC", l, pc)])

    def smp_ssd(l):
        DMA("sp", xs_q.ap, scrX[l].rearrange("b (q c) -> (b q) c", q=8), "sq0", reads=[("scrX", l)], writes=xs_q.all().keys)
        DMA("sp", B_q.ap, scrB[l].rearrange("b (q n) -> (b q) n", q=8), "sq2", reads=[("scrBC", l, 4)], writes=B_q.all().keys)
        DMA("sp", C_q.ap, scrC[l].rearrange("b (q n) -> (b q) n", q=8), "sq3", reads=[("scrBC", l, 5)], writes=C_q.all().keys)
        DMA("sp", dtq.ap[:, 0:4], scrD[l].rearrange("b (q i) -> (b q) i", q=8), "sq4", reads=[("scrD", l)], writes=dtq.all().keys)
        TT("dve", dtq.v((4, 8)), dtq.v((0, 4)), aq.v(l, None), ALU.mult)
        ACT(dtq.v((4, 8)), dtq.v((4, 8)), AF.Exp)
        TT("dve", xdt_q.all(), xs_q.all(), dtq.v((0, 4)), ALU.mult,
           in0_ap=xs_q.ap.rearrange("p (i c) -> p i c", i=4), in1_ap=bc_last(dtq.v((0, 4)), HP),
           out_ap=xdt_q.ap.rearrange("p (i c) -> p i c", i=4))
        pieces = [(hi, pp) for hi in range(4) for pp in range(HP // PP)]

        def rng(n):
            hi, pp = pieces[n]
            lo = (pp * PP) * NST
            return hi, pp, lo, lo + PP * NST

        def load(n):
            hi, pp, lo, hi_ = rng(n)
            A = bufA[n % 3]
            DMA("sp", A.ap.rearrange("p a n -> p (a n)"), sssd_d[l][:, hi, lo:hi_], f"sa{n % 3}", writes=A.all().keys)

        load(0)
        load(1)
        for n in range(len(pieces)):
            hi, pp, lo, hi_ = rng(n)
            if n + 2 < len(pieces):
                load(n + 2)
            A = bufA[n % 3]
            Bf = bufB[n % 2]
            c0 = hi * HP + pp * PP
            for i4 in range(PP):
                ACT(Bf.v(i4, None), B_q.all(), AF.Copy, scale=V(xdt_q.ap[:, c0 + i4:c0 + i4 + 1], xdt_q.all().keys))
            STT("dve", A.all(), A.all(), dtq.v((4 + hi, 5 + hi)), Bf.all(), ALU.mult, ALU.add)
            DMA("sp", ssds_d[l][:, hi, lo:hi_], A.ap.rearrange("p a n -> p (a n)"), f"so{n % 3}", reads=A.all().keys)
            TT("pool", Bf.all(), A.all(), C_q.all(), ALU.mult, in1_ap=bc_mid(C_q.all(), PP))
            p.add("dve", lambda e, o=y_q.ap[:, c0:c0 + PP], i=Bf.ap:
                  e.tensor_reduce(out=o, in_=i, axis=AX.X, op=ALU.add), reads=Bf.all().keys, writes=y_q.all().keys)
            yield
        DMA("sp", zs_q.ap, scrZ[l].rearrange("b (q c) -> (b q) c", q=8), "sq1", reads=[("scrZ", l)], writes=zs_q.all().keys)
        TT("dve", xdt_q.all(), xs_q.all(), dq.v(l, None), ALU.mult,
           in0_ap=xs_q.ap.rearrange("p (i c) -> p i c", i=4), in1_ap=bc_last(dq.v(l, None), HP),
           out_ap=xdt_q.ap.rearrange("p (i c) -> p i c", i=4))
        TT("dve", y_q.all(), y_q.all(), xdt_q.all(), ALU.add)
        TT("dve", y_q.all(), y_q.all(), zs_q.all(), ALU.mult)
        MEMSET("dve", dtq.v((8, 9)), 0.0)
        ACT(xdt_q.all(), y_q.all(), AF.Square, accum=dtq.v((8, 9)))
        MM(PS(6, 0, 1), cf32.v(4, None), dtq.v((8, 9)))
        ACT(dtq.v((10, 11)), PS(6, 0, 1), AF.Sqrt, bias=eps_c, scale=1.0 / DI)
        p.add("dve", lambda e, o=dtq.ap[:, 11:12], i=dtq.ap[:, 10:11]: e.reciprocal(out=o, in_=i),
              reads=dtq.all().keys, writes=dtq.all().keys)
        TS("dve", y_q.all(), y_q.all(), dtq.v((11, 12)), None, ALU.mult)
        for kc in range(16):
            hq_, hh = divmod(kc, 2)
            MM(PS(7, kc * NSMP, kc * NSMP + NSMP), y_q.v((hh * 128, hh * 128 + 128)),
               V(cf32.ap[:, 5, hq_ * NSMP:(hq_ + 1) * NSMP], cf32.v(5, None).keys))
        TT("dve", yznTs.all(), PS(7, 0, 256), pcol.v(l, (NORMW, NORMW + 16)), ALU.mult,
           in0_ap=PSA[:, 7, 0:256].rearrange("p (a b) -> p a b", a=16),
           in1_ap=bc_last(pcol.v(l, (NORMW, NORMW + 16)), NSMP))
        yield

    def smp_advance(gen, n):
        if gen is None:
            return
        for _ in range(n):
            try:
                next(gen)
            except StopIteration:
                return

    if n_tiles >= 4:
        IN_AT = {0: (1, 0), 1: (2, 1)}
        OUT_AT = {0: (2, 0), 1: (3, 1)}
    else:
        IN_AT = {0: (0, 0), 1: (0, 1)}
        OUT_AT = {0: (0, 0), 1: (0, 1)}
    G = {"gen": None}

    def tile_layer(t, l):
        last_tile = (t == n_tiles - 1)
        smp_in = with_samples and IN_AT[l] == (t, l)
        smp_out = with_samples and OUT_AT[l] == (t, l)
        tick = lambda n=1: smp_advance(G["gen"], n)
        tick_lo = (lambda n=1: None) if smp_in else tick
        p.tag = f"t{t}l{l}:in"
        if l == 0 and smp_in:
            DMA("sp", xstage[0].ap[0:NSMP, :], xs_d, "xs0", writes=xstage[0].all().keys)
            for k in range(KC):
                TR(PS(4, k * NSMP, k * NSMP + NSMP), V(xstage[0].ap[0:NSMP, k * 128:(k + 1) * 128], xstage[0].all().keys),
                   V(cf32.ap[0:NSMP, 0, 0:NSMP], ident.keys))
            CP("dve", xT.v(None, SC), PS(4, 0, KC * NSMP), in_ap=PSA[:, 4, 0:KC * NSMP].rearrange("p (a b) -> p a b", a=KC))
        if l == 0:
            for c in range(NCH):
                q = c % 2
                r0 = t * T + c * 128
                DMA("sp", xstage[q].ap, xp_d[r0:r0 + 128, :], f"xs{q}", writes=xstage[q].all().keys)
                for hf in range(2):
                    for kk in range(4):
                        k = hf * 4 + kk
                        TR(PS(4 + hf, kk * 128, kk * 128 + 128), xstage[q].v((k * 128, k * 128 + 128)), ident)
                    CP("act" if hf == 0 else "dve", xT.v((hf * 4, hf * 4 + 4), (c * 128, c * 128 + 128)), PS(4 + hf),
                       in_ap=PSA[:, 4 + hf, :].rearrange("p (a b) -> p a b", a=4))
        p.tag = f"t{t}l{l}:ph0"
        for j in range(KC):
            TS("dve" if j % 2 == 0 else "pool", hT.v(j, (0, T)), xT.v(j, (0, T)), modc(l, 8 + j), modc(l, j),
               ALU.mult, ALU.add)
        if smp_in:
            TT("dve", stmp.all(), xT.v(None, SC), mod.v(l, (8, 16), (1, 17)), ALU.mult)
            TT("dve", hT.v(None, SC), stmp.all(), mod.v(l, (0, 8), (1, 17)), ALU.add)
        p.tag = f"t{t}l{l}:ph1"
        for c in range(NCH):
            for k in range(KC):
                MM(PS(4, c * 32, c * 32 + 32), hT.v(k, (c * 128, c * 128 + 128)), wdt.v(l, k, None),
                   start=(k == 0), stop=(k == KC - 1))
        dtv = V(dtall.ap[:, 0:NCH, :], dtall.all().keys)
        TT("dve", dtv, PS(4, 0, 128), prow.v(l, (0, 32)), ALU.add,
           in0_ap=PSA[:, 4, 0:128].rearrange("p (a b) -> p a b", a=NCH), in1_ap=bc_mid(prow.v(l, (0, 32)), NCH))
        dav = daall.all()
        STT("dve", dav, dtv, -1.0, dtv, ALU.mult, ALU.max)
        ACT(dav, dav, AF.Exp, scale=-1.0)
        ACT(dav, dav, AF.Ln, bias=1.0)
        STT("dve", dtv, dtv, 0.0, dav, ALU.max, ALU.add)
        TT("dve", dav, dtv, abc.v(l, None), ALU.mult, in1_ap=bc_mid(abc.v(l, None), NCH))
        if smp_in:
            for k in range(KC):
                MM(PS(5, 0, 32, part=(0, NSMP)), hT.v(k, SC), wdt.v(l, k, None), start=(k == 0), stop=(k == KC - 1))
            dts = V(dtall.ap[0:NSMP, NCH, :], dtall.all().keys)
            dts2 = V(stmp2.ap[0:NSMP, 0:2, :].rearrange("p a b -> p (a b)"), stmp2.all().keys)
            TT("dve", dts, PS(5, 0, 32, part=(0, NSMP)), V(prow.ap[0:NSMP, l, 0:32], prow.v(l, (0, 32)).keys), ALU.add)
            STT("dve", dts2, dts, -1.0, dts, ALU.mult, ALU.max)
            ACT(dts2, dts2, AF.Exp, scale=-1.0)
            ACT(dts2, dts2, AF.Ln, bias=1.0)
            STT("dve", dts, dts, 0.0, dts2, ALU.max, ALU.add)
            DMA("sp", scrD[l], dts.ap, "sg0", reads=dts.keys, writes=[("scrD", l)])
        pend = []
        pendS = []

        def xbc_slab(s):
            sb = slab_next()
            for jj in range(4):
                j = 4 * s + jj
                b = nextbank()
                for k in range(KC):
                    MM(PS(b), slab_v(sb, KC, 512, k, jj * 128, jj * 128 + 128), hT.v(k, (0, T)),
                       start=(k == 0), stop=(k == KC - 1))
                if smp_in:
                    smp_ride_fm(sb, KC, 512, jj * 128, lambda k: hT.v(k, SC),
                                lambda pv, j=j: CP("act", rawS.v(j, None), pv))
                q = j % NRB
                CP("dve", raw[q].v((0, 3)), hist.v(l, j, None))
                CP("act", raw[q].v((3, 515)), PS(b))
                cw = lambda kk, l=l, j=j: pcol.v(l, (CONVW + kk * 24 + j, CONVW + kk * 24 + j + 1))
                ACT(acc[q].all(), PS(b), AF.Copy, scale=cw(3))
                for kk in range(3):
                    STT("dve", acc[q].all(), raw[q].v((kk, kk + T)), cw(kk), acc[q].all(), ALU.mult, ALU.add)
                CP("pool", hist.v(l, j, None), raw[q].v((512, 515)))
                if j < 16:
                    dst = xsTk[q].all()
                elif j < 20:
                    dst = bmT.v(j - 16, None)
                else:
                    dst = cmT.v(j - 20, None)
                pendS.append(lambda dst=dst, q=q, j=j: ACT(dst, acc[q].all(), AF.Silu,
                                                          bias=pcol.v(l, (CONVB + j, CONVB + j + 1))))
                if len(pendS) > 1:
                    pendS.pop(0)()
                tick()
                if j < 20:
                    def do_tr(j=j, dst=dst):
                        pb = 6 + (j % 2)
                        psb = V(PSA[:, pb, 0:256].bitcast(BF16), [("ps", pb)])
                        for c in range(NCH):
                            TR(V(psb.ap[:, c * 128:(c + 1) * 128], psb.keys),
                               V(dst.ap[:, c * 128:(c + 1) * 128], dst.keys), identb)
                        if j < 16:
                            o = xs_tm.v(None, (j * 128, j * 128 + 128))
                        else:
                            o = bm_tm.v(None, ((j - 16) * 128, (j - 16) * 128 + 128))
                        CP("dve", o, psb, in_ap=psb.ap.rearrange("p (a b) -> p a b", a=NCH))
                    pend.append(do_tr)
                    if len(pend) > 4:
                        pend.pop(0)()
            if smp_in:
                def ev_raw(pv, s=s):
                    CP("act", stage16(s % 2), pv)
                    DMA("sp", convs_d[l][:, 2, s * 512:(s + 1) * 512], stage16(s % 2).ap, f"ss{s % 2}",
                        reads=stage16(s % 2).keys)
                smp_ride_tm(sb, ev_raw)
            slab_done()
        def z_slab(s):
            sb = slab_next()
            for c in range(NCH):
                b = nextbank()
                for k in range(KC):
                    MM(PS(b), hT.v(k, (c * 128, c * 128 + 128)), slab_v(sb, KC, 512, k, 0, 512),
                       start=(k == 0), stop=(k == KC - 1))
                ACT(zs.v(c, (s * 512, s * 512 + 512)), PS(b), AF.Silu)
            if smp_in:
                def ev_z(pv, s=s):
                    ACT(stage16(s % 2), pv, AF.Silu)
                    DMA("sp", scrZ[l][:, s * 512:(s + 1) * 512], stage16(s % 2).ap, f"ss{s % 2}",
                        reads=stage16(s % 2).keys, writes=[("scrZ", l)])
                smp_ride_tm(sb, ev_z)
            slab_done()
        for s in range(4):
            xbc_slab(s)
            z_slab(s)
        xbc_slab(4)
        xbc_slab(5)
        while pendS:
            pendS.pop(0)()
        while pend:
            pend.pop(0)()
        if smp_in:
            smp_conv(l)
            G["gen"] = smp_ssd(l)
        p.tag = f"t{t}l{l}:ssd"
        DMA("sp", STf.ap, scrST[l], "stl", reads=[("scrST", l)], writes=STf.all().keys)
        CP("act", STb.all(), STf.all())
        ACT(lndt.all(), dtv, AF.Ln)
        items = [(c, hg) for c in range(NCH) for hg in range(8)]

        def chunk_prep(c):
            cp_ = c % 2
            tok = (c * 128, c * 128 + 128)
            da = daall.v(c, None)
            MM(PS(5, 0, 32), triu, da)
            MM(PS(5, 32, 64), trilS, da)
            MM(PS(5, 64, 96), ones, da)
            ACT(Eexp[cp_].all(), PS(5, 0, 96), AF.Exp)
            TT("dve", w2[cp_].all(), dtall.v(c, None), Eexp[cp_].v((32, 64)), ALU.mult)
            for g in range(NG):
                MM(PS(6, g * 128, g * 128 + 128), bmT.v(g, tok), cmT.v(g, tok))
            TT("dve", cbm[cp_].all(), PS(6), triu, ALU.mult,
               in0_ap=PSA[:, 6, :].rearrange("p (a b) -> p a b", a=NG), in1_ap=bc_mid(triu, NG))

        def prepR(i):
            c, hg = items[i]
            h0 = 4 * hg
            TT("pool", Rg[i % 3].all(), triu, daall.v(c, None), ALU.mult, in0_ap=bc_mid(triu, 4),
               in1_ap=bc_last(V(daall.ap[:, c, h0:h0 + 4], daall.v(c, None).keys), 128))

        def prep(i):
            c, hg = items[i]
            par = i % 2
            g = hg // 2
            h0 = 4 * hg
            MM(PS(par), trilSb, V(Rg[i % 3].ap.rearrange("p a b -> p (a b)"), Rg[i % 3].all().keys))
            for e4 in range(4):
                ACT(Lx[par].v(e4, None), PS(par, e4 * 128, e4 * 128 + 128), AF.Exp,
                    bias=V(lndt.ap[:, c, h0 + e4:h0 + e4 + 1], lndt.v(c, None).keys))
            TT("dve", Lx[par].all(), Lx[par].all(), cbm[c % 2].v(g, None), ALU.mult, in1_ap=bc_mid(cbm[c % 2].v(g, None), 4))

        def main(i):
            c, hg = items[i]
            par = i % 2
            cp_ = c % 2
            g = hg // 2
            h0 = 4 * hg
            tok = (c * 128, c * 128 + 128)
            cols = (hg * 256, hg * 256 + 256)
            yb = 2 + par
            for e4 in range(4):
                h = h0 + e4
                MM(PS(yb, e4 * 64, e4 * 64 + 64), Lx[par].v(e4, None), xs_tm.v(c, (h * HP, h * HP + HP)))
            MM(PS(yb, 256, 512), cmT.v(g, tok), STb.v(cols))
            v4 = lambda ap: ap.rearrange("p (a b) -> p a b", a=4)
            TT("dve", t1[i % 3].all(), PS(yb, 256, 512), Eexp[cp_].v((h0, h0 + 4)), ALU.mult,
               in0_ap=v4(PSA[:, yb, 256:512]), in1_ap=bc_last(Eexp[cp_].v((h0, h0 + 4)), HP), out_ap=v4(t1[i % 3].ap))
            TT("dve", tD[0].all(), xs_tm.v(c, cols), prow.v(l, (64 + h0, 64 + h0 + 4)), ALU.mult,
               in0_ap=v4(xs_tm.ap[:, c, cols[0]:cols[1]]), in1_ap=bc_last(prow.v(l, (64 + h0, 64 + h0 + 4)), HP),
               out_ap=v4(tD[0].ap))
            TT("dve", t1[i % 3].all(), PS(yb, 0, 256), t1[i % 3].all(), ALU.add)
            TT("dve", t1[i % 3].all(), t1[i % 3].all(), tD[0].all(), ALU.add)
            if i > 0:
                gate(i - 1)

        def gate(i):
            c, hg = items[i]
            cols = (hg * 256, hg * 256 + 256)
            TT("pool", yzb[c % 2].v(cols), t1[i % 3].all(), zs.v(c, cols), ALU.mult)

        v8 = lambda ap: ap.rearrange("p (a b) -> p a b", a=8)

        def su1(c, g):
            cp_ = c % 2
            gc = (g * 512, g * 512 + 512)
            hs = (8 * g, 8 * g + 8)
            TT("pool", xdtd[0].all(), xs_tm.v(c, gc), w2[cp_].v(hs), ALU.mult,
               in0_ap=v8(xs_tm.ap[:, c, gc[0]:gc[1]]), in1_ap=bc_last(w2[cp_].v(hs), HP), out_ap=v8(xdtd[0].ap))
            TT("pool", STf.v(gc), STf.v(gc), Eexp[cp_].v((64 + hs[0], 64 + hs[1])), ALU.mult,
               in0_ap=v8(STf.ap[:, gc[0]:gc[1]]), in1_ap=bc_last(Eexp[cp_].v((64 + hs[0], 64 + hs[1])), HP),
               out_ap=v8(STf.ap[:, gc[0]:gc[1]]))

        def su2(c, g):
            cp_ = c % 2
            gc = (g * 512, g * 512 + 512)
            hs = (8 * g, 8 * g + 8)
            MM(PS(4), bm_tm.v(c, (g * 128, g * 128 + 128)), xdtd[0].all())
            TT("dve", STf.v(gc), PS(4), STf.v(gc), ALU.add)

        def su3(c, g):
            gc = (g * 512, g * 512 + 512)
            CP("act", STb.v(gc), STf.v(gc))

        def ce_steps(c):
            cp_ = c % 2
            sq_ = ssq[cp_]
            tok = (c * 128, c * 128 + 128)
            steps = []

            def sq(g):
                if g == 0:
                    MEMSET("dve", sq_.all(), 0.0)
                ACT(junkb.all(), yzb[cp_].v((g * 512, g * 512 + 512)), AF.Square, accum=sq_.v((g, g + 1)))

            def stats():
                p.add("dve", lambda e, o=sq_.ap[:, 8:9], i=sq_.ap[:, 0:8]: e.tensor_reduce(out=o, in_=i, axis=AX.X, op=ALU.add),
                      reads=sq_.all().keys, writes=sq_.all().keys)
                ACT(sq_.v((9, 10)), sq_.v((8, 9)), AF.Sqrt, bias=eps_c, scale=1.0 / DI)
                p.add("dve", lambda e, o=sq_.ap[:, 10:11], i=sq_.ap[:, 9:10]: e.reciprocal(out=o, in_=i),
                      reads=sq_.all().keys, writes=sq_.all().keys)
                TS("dve", Dg.all(), identb, sq_.v((10, 11)), None, ALU.mult)

            def nt(q4):
                nb = 7 - (q4 % 2)
                for kk in range(4):
                    kc = 4 * q4 + kk
                    MM(PS(nb, kk * 128, kk * 128 + 128), yzb[cp_].v((kc * 128, kc * 128 + 128)), Dg.all())
                TT("dve", yznT.v((4 * q4, 4 * q4 + 4), tok), PS(nb), pcol.v(l, (NORMW + 4 * q4, NORMW + 4 * q4 + 4)), ALU.mult,
                   in0_ap=PSA[:, nb, :].rearrange("p (a b) -> p a b", a=4),
                   in1_ap=bc_last(pcol.v(l, (NORMW + 4 * q4, NORMW + 4 * q4 + 4)), 128))
            for g in range(NG):
                steps.append(lambda g=g: sq(g))
            steps.append(stats)
            for q4 in range(4):
                steps.append(lambda q4=q4: nt(q4))
            return steps

        todo = {}
        for c in range(NCH):
            for g in range(NG):
                n0 = c * 8 + 2 * g + 1
                todo.setdefault(n0 + 2, []).append(lambda c=c, g=g: su2(c, g))
                todo.setdefault(n0, []).append(lambda c=c, g=g: su1(c, g))
                todo.setdefault(n0 + 3, []).append(lambda c=c, g=g: su3(c, g))
        chunk_prep(0)
        prepR(0)
        prepR(1)
        prep(0)
        for i in range(len(items)):
            c, hg = items[i]
            if hg == 2 and c + 1 < NCH:
                chunk_prep(c + 1)
            if i + 2 < len(items):
                prepR(i + 2)
            if i + 1 < len(items):
                prep(i + 1)
            main(i)
            for fn in todo.pop(i, []):
                fn()
            if c > 0:
                if hg == 0:
                    cesteps = ce_steps(c - 1)
                cesteps.pop(0)()
                if hg == 7:
                    cesteps.pop(0)()
        gate(len(items) - 1)
        for i in sorted(todo):
            for fn in todo[i]:
                fn()
        for fn in ce_steps(NCH - 1):
            fn()
        DMA("sp", scrST[l], STf.ap, f"sts{l}", reads=STf.all().keys, writes=[("scrST", l)])
        if last_tile:
            for q4 in range(4):
                for kk in range(4):
                    kc = 4 * q4 + kk
                    TR(PS(4 + (q4 % 2), kk * 128, kk * 128 + 128), STf.v((kc * 128, kc * 128 + 128)), ident)
                yq = ystage_a[q4 % 2]
                CP("act", yq.v((0, 512)), PS(4 + (q4 % 2)))
                DMA("sp", ssdp_d[l].rearrange("(a p) n -> p a n", p=128)[:, 4 * q4:4 * q4 + 4, :],
                    yq.ap[:, 0:512].rearrange("p (a n) -> p a n", a=4), f"ys{q4 % 2}", reads=yq.v((0, 512)).keys)
            TR(PS(6, 0, 128, part=(0, 72)), V(hist.ap[:, l, :, :].rearrange("p a b -> p (a b)"), hist.v(l, None, None).keys), ident)
            CP("dve", V(tmix.ap[0:72, 0, :], tmix.v(0, None).keys), PS(6, 0, 128, part=(0, 72)))
            for kq in range(24):
                DMA("sp", convp_d[l][:, kq * 128:(kq + 1) * 128], tmix.ap[kq * 3:kq * 3 + 3, 0, :], "fin",
                    reads=tmix.v(0, None).keys)
        p.tag = f"t{t}l{l}:uv"
        for s in range(2):
            sb = slab_next()
            for jj in range(4):
                j = 4 * s + jj
                b = nextbank()
                for k in range(KC):
                    MM(PS(b), slab_v(sb, KC, 512, k, jj * 128, jj * 128 + 128), hT.v(k, (0, T)),
                       start=(k == 0), stop=(k == KC - 1))
                ACT(uT.v(j, (0, T)), PS(b), AF.Gelu)
                if smp_in:
                    smp_ride_fm(sb, KC, 512, jj * 128, lambda k: hT.v(k, SC),
                                lambda pv, j=j: ACT(uT.v(j, SC), pv, AF.Gelu))
                tick_lo()
            slab_done()
        sv = [slab_next(), slab_next()]
        for c in range(NCH):
            q = c % 2
            MEMSET("dve", vstat.all(), 0.0)
            for s in range(2):
                b = nextbank()
                for k in range(KC):
                    MM(PS(b), hT.v(k, (c * 128, c * 128 + 128)), slab_v(sv[s], KC, 512, k, 0, 512),
                       start=(k == 0), stop=(k == KC - 1))
                ACT(gv[q].v((s * 512, s * 512 + 512)), PS(b), AF.Gelu, accum=vstat.v((s, s + 1)))
            ACT(junk2.all(), gv[q].all(), AF.Square, accum=vstat.v((2, 3)))
            TT("dve", vstat.v((3, 4)), vstat.v((0, 1)), vstat.v((1, 2)), ALU.add)
            TS("dve", vstat.v((3, 4)), vstat.v((3, 4)), 1.0 / D, None, ALU.mult)
            TT("dve", vstat.v((4, 5)), vstat.v((3, 4)), vstat.v((3, 4)), ALU.mult)
            STT("dve", vstat.v((5, 6)), vstat.v((2, 3)), 1.0 / D, vstat.v((4, 5)), ALU.mult, ALU.subtract)
            ACT(vstat.v((6, 7)), vstat.v((5, 6)), AF.Sqrt, bias=eps_c, scale=1.0)
            p.add("dve", lambda e, o=vstat.ap[:, 7:8], i=vstat.ap[:, 6:7]: e.reciprocal(out=o, in_=i),
                  reads=vstat.all().keys, writes=vstat.all().keys)
            TS("dve", vln.v(c, None), gv[q].all(), vstat.v((3, 4)), vstat.v((7, 8)), ALU.subtract, ALU.mult)
        if smp_in:
            g16 = V(gv[0].ap[0:NSMP, :], gv[0].all().keys)
            vs16 = lambda a, b_: V(vstat.ap[0:NSMP, a:b_], vstat.all().keys)
            MEMSET("dve", vstat.all(), 0.0)
            DMA("sp", lnvg_s.ap[0:NSMP, :], lnvbc_d[:, (l * 2) * D:(l * 2 + 1) * D], "sg1", writes=lnvg_s.all().keys)
            DMA("sp", lnvb_s.ap[0:NSMP, :], lnvbc_d[:, (l * 2 + 1) * D:(l * 2 + 2) * D], "sg2", writes=lnvb_s.all().keys)
            for s in range(2):
                b = sbank()
                for k in range(KC):
                    MM(PS(b, 0, 512, part=(0, NSMP)), hT.v(k, SC), slab_v(sv[s], KC, 512, k, 0, 512),
                       start=(k == 0), stop=(k == KC - 1))
                ACT(V(gv[0].ap[0:NSMP, s * 512:(s + 1) * 512], gv[0].v((s * 512, s * 512 + 512)).keys),
                    PS(b, 0, 512, part=(0, NSMP)), AF.Gelu, accum=vs16(s, s + 1))
            ACT(V(junk2.ap[0:NSMP, :], junk2.all().keys), g16, AF.Square, accum=vs16(2, 3))
            TT("dve", vs16(3, 4), vs16(0, 1), vs16(1, 2), ALU.add)
            TS("dve", vs16(3, 4), vs16(3, 4), 1.0 / D, None, ALU.mult)
            TT("dve", vs16(4, 5), vs16(3, 4), vs16(3, 4), ALU.mult)
            STT("dve", vs16(5, 6), vs16(2, 3), 1.0 / D, vs16(4, 5), ALU.mult, ALU.subtract)
            ACT(vs16(6, 7), vs16(5, 6), AF.Sqrt, bias=V(smalls.ap[0:NSMP, 1:2], smalls.all().keys), scale=1.0)
            p.add("dve", lambda e, o=vstat.ap[0:NSMP, 7:8], i=vstat.ap[0:NSMP, 6:7]: e.reciprocal(out=o, in_=i),
                  reads=vstat.all().keys, writes=vstat.all().keys)
            TS("dve", g16, g16, vs16(3, 4), vs16(7, 8), ALU.subtract, ALU.mult)
            TT("dve", g16, g16, V(lnvg_s.ap[0:NSMP, :], lnvg_s.all().keys), ALU.mult)
            TT("dve", g16, g16, V(lnvb_s.ap[0:NSMP, :], lnvb_s.all().keys), ALU.add)
            DMA("sp", vs_d[l], g16.ap, "svs", reads=g16.keys)
            for k in range(KC):
                TR(PS(4, k * NSMP, k * NSMP + NSMP), V(gv[0].ap[0:NSMP, k * 128:(k + 1) * 128], gv[0].all().keys),
                   V(cf32.ap[0:NSMP, 0, 0:NSMP], ident.keys))
            CP("dve", vlnTs.all(), PS(4, 0, KC * NSMP), in_ap=PSA[:, 4, 0:KC * NSMP].rearrange("p (a b) -> p a b", a=KC))
        slab_done()
        slab_done()
        tick_lo(4)
        p.tag = f"t{t}l{l}:mix"
        def mix_chunk(c):
            for g in range(8):
                hf, gg = divmod(g, 4)
                MM(PS(6 + hf, gg * 128, gg * 128 + 128), vln.v(c, (g * 128, g * 128 + 128)), wsTb.v(l, g, None))
            for g in range(8):
                hf, gg = divmod(g, 4)
                STT("dve", tmix.v(g, None), PS(6 + hf, gg * 128, gg * 128 + 128), pcol.v(l, (LNVG + g, LNVG + g + 1)),
                    BIASb.v(l, g, None), ALU.mult, ALU.add)
            TT("pool", uT.v(None, (c * 128, c * 128 + 128)), tmix.all(), uT.v(None, (c * 128, c * 128 + 128)), ALU.mult)
        if smp_in:
            TT("dve", stmp.all(), vlnTs.all(), pcol.v(l, (WDIAG, WDIAG + 8)), ALU.mult,
               in1_ap=bc_last(pcol.v(l, (WDIAG, WDIAG + 8)), NSMP))
            TT("dve", stmp.all(), stmp.all(), pcol.v(l, (BS0, BS0 + 8)), ALU.add,
               in1_ap=bc_last(pcol.v(l, (BS0, BS0 + 8)), NSMP))
            TT("dve", ocmS.all(), stmp.all(), uT.v(None, SC), ALU.mult)
        p.tag = f"t{t}l{l}:merge"
        def gates(br):
            for s in range(2):
                sb = slab_next()
                for jj in range(4):
                    j = 4 * s + jj
                    b = nextbank()
                    for k in range(KC):
                        MM(PS(b), slab_v(sb, KC, 512, k, jj * 128, jj * 128 + 128), hT.v(k, (0, T)),
                           start=(k == 0), stop=(k == KC - 1))
                    ACT(sg.v(br, j, (0, T)), PS(b), AF.Sigmoid,
                        bias=pcol.v(l, (BGATE + br * 8 + j, BGATE + br * 8 + j + 1)))
                    if (br * 8 + j) % 3 == 0 and mixq:
                        mix_chunk(mixq.pop(0))
                    tick_lo()
                    if smp_in:
                        smp_ride_fm(sb, KC, 512, jj * 128, lambda k: hT.v(k, SC),
                                    lambda pv, j=j, br=br: ACT(sgS.v(br, j, None), pv, AF.Sigmoid,
                                                               bias=pcol.v(l, (BGATE + br * 8 + j, BGATE + br * 8 + j + 1))))
                slab_done()

        def branch(br, first):
            nk, cw, nsl, nb = (KC, 512, 2, 4) if br == 0 else (16, 256, 4, 2)
            for s in range(nsl):
                sb = slab_next()
                for jj in range(nb):
                    j = nb * s + jj
                    b = nextbank()
                    for k in range(nk):
                        rhs = uT.v(k, (0, T)) if br == 0 else yznT.v(k, None)
                        MM(PS(b), slab_v(sb, nk, cw, k, jj * 128, jj * 128 + 128), rhs, start=(k == 0), stop=(k == nk - 1))
                    q = j % 2
                    if first:
                        TT("dve", mg.v(j, (0, T)), PS(b), sg.v(br, j, (0, T)), ALU.mult)
                    else:
                        TT("dve", tmpf[q].all(), PS(b), sg.v(br, j, (0, T)), ALU.mult)
                        TT("pool", mg.v(j, (0, T)), tmpf[q].all(), mg.v(j, (0, T)), ALU.add)
                    if smp_out:
                        def ev(pv, j=j):
                            if first:
                                TT("dve", mg.v(j, SC), pv, sgS.v(br, j, None), ALU.mult)
                            else:
                                TT("dve", stmp.v(0, None), pv, sgS.v(br, j, None), ALU.mult)
                                TT("dve", mg.v(j, SC), stmp.v(0, None), mg.v(j, SC), ALU.add)
                        rfn = (lambda k: ocmS.v(k, None)) if br == 0 else (lambda k: yznTs.v(k, None))
                        smp_ride_fm(sb, nk, cw, jj * 128, rfn, ev)
                    tick_lo()
                slab_done()

        mixq = list(range(NCH))
        gates(0)
        gates(1)
        while mixq:
            mix_chunk(mixq.pop(0))
        if smp_out:
            tick(1000)
        branch(1, True)
        branch(0, False)
        p.tag = f"t{t}l{l}:wo"
        for s in range(2):
            sb = slab_next()
            if smp_out:
                for jj in range(4):
                    j = 4 * s + jj

                    def ev_wo(pv, j=j):
                        TT("dve", stmp.v(0, None), pv, mod.v(l, 16 + j, (1, 17)), ALU.mult)
                        TT("dve", xT.v(j, SC), stmp.v(0, None), xT.v(j, SC), ALU.add)
                    smp_ride_fm(sb, KC, 512, jj * 128, lambda k: mg.v(k, SC), ev_wo)
                if s == 1:
                    layer_norm_apply(l, 1, True, part="smp")
            for jj in range(4):
                j = 4 * s + jj
                b = nextbank()
                for k in range(KC):
                    MM(PS(b), slab_v(sb, KC, 512, k, jj * 128, jj * 128 + 128), mg.v(k, (0, T)),
                       start=(k == 0), stop=(k == KC - 1))
                STT("dve", xT.v(j, (0, T)), PS(b), modc(l, 16 + j), xT.v(j, (0, T)), ALU.mult, ALU.add)
                tick_lo()
            slab_done()
        layer_norm_apply(l, 1, False, part="main")
        p.tag = f"t{t}l{l}:ffn"
        for jb in range(6):
            sbg = slab_next()
            sbu = slab_next()
            nblk = 4 if jb < 5 else 2
            for jj in range(nblk):
                j = 4 * jb + jj
                bg = nextbank()
                for k in range(KC):
                    MM(PS(bg), slab_v(sbg, KC, 512, k, jj * 128, jj * 128 + 128), hT.v(k, (0, T)),
                       start=(k == 0), stop=(k == KC - 1))
                bu = nextbank()
                for k in range(KC):
                    MM(PS(bu), slab_v(sbu, KC, 512, k, jj * 128, jj * 128 + 128), hT.v(k, (0, T)),
                       start=(k == 0), stop=(k == KC - 1))
                q = j % 2
                ACT(sgt[q].v((0, T)), PS(bg), AF.Silu)
                TT("dve", actb.v(j, (0, T)), PS(bu), sgt[q].v((0, T)), ALU.mult)
                if smp_out:
                    for k in range(KC):
                        MM(PS(4, 0, NSMP), slab_v(sbg, KC, 512, k, jj * 128, jj * 128 + 128), hT.v(k, SC),
                           start=(k == 0), stop=(k == KC - 1))
                    for k in range(KC):
                        MM(PS(5, 0, NSMP), slab_v(sbu, KC, 512, k, jj * 128, jj * 128 + 128), hT.v(k, SC),
                           start=(k == 0), stop=(k == KC - 1))
                    ACT(sgt[q].v(SC), PS(4, 0, NSMP), AF.Silu)
                    TT("dve", actb.v(j, SC), PS(5, 0, NSMP), sgt[q].v(SC), ALU.mult)
                tick()
            slab_done()
            slab_done()
        for cg in range(4):
            b0 = nextbank()
            b1 = nextbank()
            bb = (b0, b1)
            for kh in range(2):
                sb = slab_next()
                for jj in range(2):
                    for k in range(11):
                        kk = kh * 11 + k
                        MM(PS(bb[jj]), slab_v(sb, 11, 256, k, jj * 128, jj * 128 + 128), actb.v(kk, (0, T)),
                           start=(kk == 0), stop=(kk == KFF - 1))
                    if smp_out:
                        for k in range(11):
                            kk = kh * 11 + k
                            MM(PS(4 + jj, 0, NSMP), slab_v(sb, 11, 256, k, jj * 128, jj * 128 + 128), actb.v(kk, SC),
                               start=(kk == 0), stop=(kk == KFF - 1))
                slab_done()
            for jj in range(2):
                j = cg * 2 + jj
                STT("dve", xT.v(j, (0, T)), PS(bb[jj]), modc(l, 40 + j), xT.v(j, (0, T)), ALU.mult, ALU.add)
                if smp_out:
                    TT("dve", stmp.v(0, None), PS(4 + jj, 0, NSMP), mod.v(l, 40 + j, (1, 17)), ALU.mult)
                    TT("dve", xT.v(j, SC), stmp.v(0, None), xT.v(j, SC), ALU.add)
                tick()
        layer_norm_apply(l, 2, smp_out)
        p.tag = f"t{t}l{l}:out"
        if l == DEPTH - 1 and smp_out:
            for k in range(KC):
                hf, kk = divmod(k, 4)
                TR(PS(4 + hf, kk * 128, kk * 128 + 128, part=(0, NSMP)), xT.v(k, SC), ident)
            for hf in range(2):
                CP("act", V(ystage[0].ap[0:NSMP, hf * 512:(hf + 1) * 512], ystage[0].v((hf * 512, hf * 512 + 512)).keys),
                   PS(4 + hf, 0, 512, part=(0, NSMP)))
            DMA("sp", ys_d, ystage[0].ap[0:NSMP, :], "ys0", reads=ystage[0].all().keys)
        if l == DEPTH - 1:
            for c in range(NCH):
                q = c % 2
                for hf in range(2):
                    for kk in range(4):
                        k = hf * 4 + kk
                        TR(PS(4 + hf, kk * 128, kk * 128 + 128), xT.v(k, (c * 128, c * 128 + 128)), ident)
                    CP("act" if hf == 0 else "dve", ystage[q].v((hf * 512, hf * 512 + 512)), PS(4 + hf))
                r0 = t * T + c * 128
                DMA("sp", yp_d[r0:r0 + 128, :], ystage[q].ap, f"ys{q}", reads=ystage[q].all().keys)

    ystage_a = [Buf(arena, tmpf[0].off, [512], F32), Buf(arena, tmpf[1].off, [512], F32)]

    for t in range(n_tiles):
        for l in range(DEPTH):
            if t == 0 and l == 1:
                p.tag = "mod1"
                compute_mod(1)
            tile_layer(t, l)

    p.emit(sems, dsems)
    st.close()
    return nc, p


def _consts():
    c = np.zeros((128, 6, 128), np.float32)
    c[:, 0, :] = np.eye(128, dtype=np.float32)
    c[:, 1, :] = np.triu(np.ones((128, 128), np.float32))
    c[:, 2, :] = np.tril(np.ones((128, 128), np.float32), -1)
    c[:, 3, :] = 1.0
    q = np.arange(128)
    c[:, 4, :] = (q[:, None] // 8 == q[None, :] // 8).astype(np.float32)
    for hq in range(8):
        for b in range(16):
            c[b * 8 + hq, 5, hq * 16 + b] = 1.0
    return c.reshape(128, 6 * 128)


def _shared_inputs(inp):
    f = lambda k: np.asarray(inp[k], dtype=np.float32)
    w_in = f("w_in")
    wada = np.zeros((DEPTH, 12, 128, SLAB), np.float32)
    wstr = np.zeros((DEPTH, NSLAB, 128, SLAB), np.float32)
    wdt = np.zeros((128, DEPTH, KC, 32), np.float32)
    pcol = np.zeros((128, DEPTH, NPC), np.float32)
    prow = np.zeros((128, DEPTH, 96), np.float32)
    pq = np.zeros((128, DEPTH, 8), np.float32)
    wst = np.zeros((128, DEPTH, 8, 128), np.float32)
    bsb = np.zeros((128, DEPTH, 1024), np.float32)
    lnvbc = np.zeros((NSMP, DEPTH, 2, 1024), np.float32)
    for l in range(DEPTH):
        for s in range(12):
            wada[l, s] = _slab_pack(f("w_ada")[l], s * 512, 512, 0, KC)
        i = 0
        order = [("x", 0), ("z", 0), ("x", 1), ("z", 1), ("x", 2), ("z", 2), ("x", 3), ("z", 3), ("x", 4), ("x", 5),
                 ("u", 0), ("u", 1), ("v", 0), ("v", 1)]
        base = {"x": 4096, "z": 2048, "u": 0, "v": 1024}
        for nm, s_ in order:
            wstr[l, i] = _slab_pack(w_in[l], base[nm] + s_ * 512, 512, 0, KC)
            i += 1
        for s in range(2):
            wstr[l, i] = _slab_pack(w_in[l], 7200 + s * 512, 512, 0, KC); i += 1
        for s in range(2):
            wstr[l, i] = _slab_pack(w_in[l], 8224 + s * 512, 512, 0, KC); i += 1
        for s in range(4):
            wstr[l, i] = _slab_pack(f("w_ssd_br")[l], s * 256, 256, 0, 16); i += 1
        for s in range(2):
            wstr[l, i] = _slab_pack(f("w_cm_br")[l], s * 512, 512, 0, KC); i += 1
        for s in range(2):
            wstr[l, i] = _slab_pack(f("w_o")[l], s * 512, 512, 0, KC); i += 1
        for s in range(6):
            wstr[l, i] = _slab_pack(f("w_ffn_gate")[l], s * 512, 512, 0, KC); i += 1
            wstr[l, i] = _slab_pack(f("w_ffn_up")[l], s * 512, 512, 0, KC); i += 1
        for cg in range(4):
            for kh in range(2):
                wstr[l, i] = _slab_pack(f("w_ffn_down")[l], cg * 256, 256, kh * 11, 11); i += 1
        assert i == NSLAB
        wdt[:, l] = w_in[l][:, 7168:7200].reshape(KC, 128, 32).transpose(1, 0, 2)
        pcol[:, l, BADA:BADA + 48] = _pcols(f("b_ada")[l])
        pcol[:, l, BGATE:BGATE + 16] = _pcols(f("b_gate")[l])
        pcol[:, l, LNVG:LNVG + 8] = _pcols(f("ln_v_g")[l])
        pcol[:, l, LNVB:LNVB + 8] = _pcols(f("ln_v_b")[l])
        for kk in range(4):
            pcol[:, l, CONVW + kk * 24:CONVW + kk * 24 + 24] = _pcols(f("conv_w")[l, kk])
        pcol[:, l, CONVB:CONVB + 24] = _pcols(f("conv_b")[l])
        pcol[:, l, NORMW:NORMW + 16] = _pcols(f("ssd_norm_w")[l])
        pcol[:, l, LN1G:LN1G + 8] = _pcols(f("ln1_g")[l])
        pcol[:, l, LN1B:LN1B + 8] = _pcols(f("ln1_b")[l])
        pcol[:, l, LN2G:LN2G + 8] = _pcols(f("ln2_g")[l])
        pcol[:, l, LN2B:LN2B + 8] = _pcols(f("ln2_b")[l])
        pcol[:, l, WDIAG:WDIAG + 8] = np.broadcast_to(f("w_spatial")[l, :, 0, 0][None, :], (128, 8))
        pcol[:, l, BS0:BS0 + 8] = np.broadcast_to(f("b_spatial")[l, :, 0][None, :], (128, 8))
        prow[:, l, 0:32] = f("dt_bias")[l][None, :]
        prow[:, l, 32:64] = f("a_log")[l][None, :]
        prow[:, l, 64:96] = f("d_skip")[l][None, :]
        hq = np.arange(128) % 8
        for hi in range(4):
            pq[:, l, hi] = f("a_log")[l][hq * 4 + hi]
            pq[:, l, 4 + hi] = f("d_skip")[l][hq * 4 + hi]
        wst[:, l] = f("w_spatial")[l].transpose(2, 0, 1)
        bsb[:, l] = f("b_spatial")[l].reshape(1, 1024)
        lnvbc[:, l, 0] = f("ln_v_g")[l][None, :]
        lnvbc[:, l, 1] = f("ln_v_b")[l][None, :]
    return {
        "wada": wada, "wstr": wstr, "wdt": wdt.reshape(128, -1), "pcol": pcol.reshape(128, -1),
        "prow": prow.reshape(128, -1), "pq": pq.reshape(128, -1), "wst": wst.reshape(128, -1),
        "bsb": bsb.reshape(128, -1), "lnvbc": lnvbc.reshape(NSMP, -1), "consts": _consts(),
    }


_CACHE = {}


def kernel(**inp):
    f = lambda k: np.asarray(inp[k], dtype=np.float32)
    shared = _shared_inputs(inp)
    in_maps = []
    for i in range(NCORES):
        m = dict(shared)
        m["xp"] = np.ascontiguousarray(f("x_prompt")[i])
        m["xs"] = np.ascontiguousarray(f("x_sample")[16 * i:16 * i + 16, 0, :])
        m["cc"] = np.ascontiguousarray(np.concatenate([f("c_prompt")[i:i + 1], f("c_sample")[16 * i:16 * i + 16]], 0))
        m["sssd"] = np.ascontiguousarray(f("state_ssd")[:, 16 * i:16 * i + 16]).reshape(DEPTH, NSMP * 8, 4, HP * NST)
        m["sconv"] = np.ascontiguousarray(f("state_conv")[:, 16 * i:16 * i + 16]).reshape(DEPTH, NSMP * 3, CONV)
        in_maps.append(m)
    if "nc" not in _CACHE:
        _CACHE["nc"] = build_program()[0]
    nc = _CACHE["nc"]
    res = run_bass_kernel_spmd(nc, in_maps, core_ids=list(range(NCORES)))
    R = res.results
    y_prompt = np.stack([R[i]["yp"] for i in range(NCORES)], 0)
    y_sample = np.concatenate([R[i]["ys"] for i in range(NCORES)], 0).reshape(128, 1, D)
    ssd_p = np.stack([R[i]["ssdp"].reshape(DEPTH, NH, HP, NST) for i in range(NCORES)], 1)
    conv_p = np.stack([R[i]["convp"] for i in range(NCORES)], 1)
    ssd_s = np.concatenate([R[i]["ssds"].reshape(DEPTH, NSMP, NH, HP, NST) for i in range(NCORES)], 1)
    conv_s = np.concatenate([R[i]["convs"] for i in range(NCORES)], 1)
    v_s = np.concatenate([R[i]["vs"] for i in range(NCORES)], 1).reshape(DEPTH, 128, 1, D)
    return (y_prompt, y_sample, ssd_p, conv_p, ssd_s, conv_s, v_s)
```

```python
import contextlib
import numpy as np
import concourse.bass as bass
import concourse.mybir as mybir
from concourse.bass_utils import run_bass_kernel_spmd

F32 = mybir.dt.float32
BF16 = mybir.dt.bfloat16
AF = mybir.ActivationFunctionType
ALU = mybir.AluOpType
AX = mybir.AxisListType

NCORES = 8
D = 1024
KC = 8
SEQ = 2048
T = 512
NCH = 4
NTILES = 4
NSMP = 16
W = T + NSMP
DI = 2048
NH = 32
HP = 64
NG = 4
NST = 128
CONV = 3072
DFF = 2816
KFF = 22
DEPTH = 2
ALPHA = 4.0 ** 0.25
EPS = 1e-5
EPS_LN = EPS / (ALPHA * ALPHA)
NSLAB = 46
SLAB = 4096
NSL = 4

BADA, BGATE, LNVG, LNVB, CONVW, CONVB, NORMW, LN1G, LN1B, LN2G, LN2B, WDIAG, BS0 = (
    0, 48, 64, 72, 80, 176, 200, 216, 224, 232, 240, 248, 256)
NPC = 264


class Op:
    __slots__ = ("eng", "fn", "deps", "signal", "count", "dsem", "dcount", "tag")

    def __init__(self, eng, fn):
        self.eng = eng
        self.fn = fn
        self.deps = []
        self.signal = False
        self.count = 0
        self.dsem = None
        self.dcount = 0


class Prog:
    ENGS = ("pe", "act", "dve", "pool", "sp")

    def __init__(self, nc):
        self.nc = nc
        self.eng_ops = {e: [] for e in self.ENGS}
        self.last_w = {}
        self.readers = {}
        self.dma_counts = {}
        self.nops = 0
        self.tag = ""
        self.names = None

    def add(self, eng, fn, reads=(), writes=(), dsem=None):
        op = Op(eng, fn)
        op.tag = self.tag
        seen = {}
        lw = self.last_w
        rd = self.readers
        for k in reads:
            w = lw.get(k)
            if w is not None:
                seen[id(w)] = (w, True)
        for k in writes:
            w = lw.get(k)
            if w is not None and id(w) not in seen:
                seen[id(w)] = (w, False)
            for r in rd.get(k, ()):
                if id(r) not in seen:
                    seen[id(r)] = (r, False)
        for d, raw in seen.values():
            if d.dsem is None and d.eng == eng:
                if eng == "pe":
                    continue
            op.deps.append(d)
            if d.dsem is None:
                d.signal = True
        if dsem is not None:
            op.dsem = dsem
            self.dma_counts[dsem] = self.dma_counts.get(dsem, 0) + 16
            op.dcount = self.dma_counts[dsem]
        for k in reads:
            rd.setdefault(k, []).append(op)
        for k in writes:
            lw[k] = op
            rd[k] = []
        self.eng_ops[eng].append(op)
        self.nops += 1
        return op

    def emit(self, sems, dsems):
        nc = self.nc
        for e in self.ENGS:
            c = 0
            for op in self.eng_ops[e]:
                if op.dsem is None and op.signal:
                    c += 1
                    op.count = c
        with nc.Block() as block:
            def run(e):
                def body(eng):
                    waited = {}
                    for op in self.eng_ops[e]:
                        need = {}
                        for d in op.deps:
                            if d.dsem is not None:
                                key = ("d", d.dsem)
                                val = d.dcount
                            else:
                                key = ("e", d.eng)
                                val = d.count
                            if val > need.get(key, 0):
                                need[key] = val
                        for key, val in need.items():
                            if waited.get(key, 0) >= val:
                                continue
                            waited[key] = val
                            s = dsems[key[1]] if key[0] == "d" else sems[key[1]]
                            eng.wait_ge(s, val)
                        ins = op.fn(eng)
                        if self.names is not None:
                            try:
                                self.names[ins.ins.name] = op.tag
                            except Exception:
                                pass
                        if op.dsem is not None:
                            ins.then_inc(dsems[op.dsem], 16)
                        elif op.signal:
                            ins.then_inc(sems[e], 1)
                    if e == "sp":
                        for name, cnt in self.dma_counts.items():
                            eng.wait_ge(dsems[name], cnt)
                return body
            block.tensor(run("pe"))
            block.scalar(run("act"))
            block.vector(run("dve"))
            block.gpsimd(run("pool"))
            block.sync(run("sp"))


class V:
    __slots__ = ("ap", "keys")

    def __init__(self, ap, keys):
        self.ap = ap
        self.keys = keys


KEYG = 512


class Buf:
    def __init__(self, arena, off, shape, dtype):
        self.off = off
        self.shape = tuple(shape)
        self.dtype = dtype
        self.es = 4 if dtype == F32 else 2
        n = 1
        for s in shape:
            n *= s
        self.n = n
        self.nbytes = n * self.es
        assert off % 4 == 0
        ap = arena[:, off // 2: off // 2 + (self.nbytes + 1) // 2]
        if dtype == F32:
            ap = ap.bitcast(F32)
        if len(shape) == 2:
            ap = ap.rearrange("p (a b) -> p a b", a=shape[0])
        elif len(shape) == 3:
            ap = ap.rearrange("p (a b c) -> p a b c", a=shape[0], b=shape[1])
        self.ap = ap

    def _keys(self, lo, hi):
        b0 = (self.off + lo * self.es) // KEYG
        b1 = (self.off + hi * self.es - 1) // KEYG
        return range(b0, b1 + 1)

    def v(self, *idx, part=None):
        shape = self.shape
        assert len(idx) == len(shape)
        sl = []
        rngs = []
        for i, s in zip(idx, shape):
            if i is None:
                sl.append(slice(None))
                rngs.append((0, s))
            elif isinstance(i, tuple):
                sl.append(slice(i[0], i[1]))
                rngs.append(i)
            else:
                sl.append(i)
                rngs.append((i, i + 1))
        psl = slice(None) if part is None else slice(part[0], part[1])
        ap = self.ap[(psl,) + tuple(sl)]
        keys = set()
        if len(shape) == 1:
            keys.update(self._keys(rngs[0][0], rngs[0][1]))
        elif len(shape) == 2:
            L = shape[1]
            if rngs[1] == (0, L):
                keys.update(self._keys(rngs[0][0] * L, rngs[0][1] * L))
            else:
                for a in range(*rngs[0]):
                    keys.update(self._keys(a * L + rngs[1][0], a * L + rngs[1][1]))
        else:
            L1, L2 = shape[1], shape[2]
            for a in range(*rngs[0]):
                if rngs[2] == (0, L2):
                    keys.update(self._keys((a * L1 + rngs[1][0]) * L2, (a * L1 + rngs[1][1]) * L2))
                else:
                    for b in range(*rngs[1]):
                        base = (a * L1 + b) * L2
                        keys.update(self._keys(base + rngs[2][0], base + rngs[2][1]))
        return V(ap, list(keys))

    def all(self):
        return self.v(*([None] * len(self.shape)))


def _slab_pack(Wm, c0, cw, k0, nk):
    blk = Wm[k0 * 128:(k0 + nk) * 128, c0:c0 + cw]
    cwv = blk.shape[1]
    out = np.zeros((128, SLAB), np.float32)
    tmp = blk.reshape(nk, 128, cwv).transpose(1, 0, 2)
    o3 = out[:, :nk * cw].reshape(128, nk, cw)
    o3[:, :, :cwv] = tmp
    return out


def _pcols(vec):
    return np.ascontiguousarray(vec.reshape(-1, 128).T)


def build_program(n_tiles=NTILES, with_samples=True):
    nc = bass.Bass("TRN2", target_bir_lowering=False)

    def din(name, shape):
        return nc.dram_tensor(name, list(shape), F32, kind="ExternalInput").ap()

    def dout(name, shape):
        return nc.dram_tensor(name, list(shape), F32, kind="ExternalOutput").ap()

    xp_d = din("xp", [SEQ, D])
    xs_d = din("xs", [NSMP, D])
    cc_d = din("cc", [NSMP + 1, D])
    sssd_d = din("sssd", [DEPTH, NSMP * 8, 4, HP * NST])
    sconv_d = din("sconv", [DEPTH, NSMP * 3, CONV])
    wada_d = din("wada", [DEPTH, 12, 128, SLAB])
    wstr_d = din("wstr", [DEPTH, NSLAB, 128, SLAB])
    wdt_d = din("wdt", [128, DEPTH * KC * 32])
    pcol_d = din("pcol", [128, DEPTH * NPC])
    prow_d = din("prow", [128, DEPTH * 96])
    pq_d = din("pq", [128, DEPTH * 8])
    wst_d = din("wst", [128, DEPTH * 8 * 128])
    bsb_d = din("bsb", [128, DEPTH * 1024])
    lnvbc_d = din("lnvbc", [NSMP, DEPTH * 2 * 1024])
    consts_d = din("consts", [128, 6 * 128])

    yp_d = dout("yp", [SEQ, D])
    ys_d = dout("ys", [NSMP, D])
    ssdp_d = dout("ssdp", [DEPTH, DI, NST])
    convp_d = dout("convp", [DEPTH, 3, CONV])
    ssds_d = dout("ssds", [DEPTH, NSMP * 8, 4, HP * NST])
    convs_d = dout("convs", [DEPTH, NSMP, 3, CONV])
    vs_d = dout("vs", [DEPTH, NSMP, D])

    wcache = nc.dram_tensor("wcache", [DEPTH, NSLAB, 128, SLAB], BF16).ap()
    scrST = nc.dram_tensor("scrST", [DEPTH, 128, DI], F32).ap()
    scrX = nc.dram_tensor("scrX", [DEPTH, NSMP, DI], F32).ap()
    scrZ = nc.dram_tensor("scrZ", [DEPTH, NSMP, DI], F32).ap()
    scrB = nc.dram_tensor("scrB", [DEPTH, NSMP, 1024], F32).ap()
    scrC = nc.dram_tensor("scrC", [DEPTH, NSMP, 1024], F32).ap()
    scrD = nc.dram_tensor("scrD", [DEPTH, NSMP, 32], F32).ap()

    p = Prog(nc)
    st = contextlib.ExitStack()
    E = st.enter_context

    P_BYTES = 90 * 1024
    DYN_BYTES = 98 * 1024
    SMP_BYTES = 18 * 1024
    ARENA = P_BYTES + DYN_BYTES + SMP_BYTES
    arena_t = E(nc.sbuf_tensor("arena", [128, ARENA // 2], BF16))
    arena = arena_t[:]
    psum_t = E(nc.psum_tensor("psum", [128, 8, 512], F32))
    PSA = psum_t[:]

    class Bump:
        def __init__(self, base, limit):
            self.o = base
            self.limit = limit

        def __call__(self, shape, dtype, align=512):
            self.o = (self.o + align - 1) // align * align
            b = Buf(arena, self.o, shape, dtype)
            self.o += b.nbytes
            assert self.o <= self.limit, (self.o, self.limit)
            return b

    pa = Bump(0, P_BYTES)
    xT = pa([KC, W], F32)
    hT = pa([KC, W], BF16)
    mod = pa([DEPTH, 48, 17], F32)
    mod2 = pa([DEPTH, 16, 17], F32)
    pcol = pa([DEPTH, NPC], F32)
    prow = pa([DEPTH, 96], F32)
    abc = pa([DEPTH, 32], F32)
    cf32 = pa([6, 128], F32)
    cb16 = pa([4, 128], BF16)
    cM = pa([128], BF16)
    wsTb = pa([DEPTH, 8, 128], BF16)
    BIASb = pa([DEPTH, 8, 128], F32)
    wdt = pa([DEPTH, KC, 32], BF16)
    hist = pa([DEPTH, 24, 3], F32)
    csT = pa([KC, 17], BF16)
    smalls = pa([64], F32)
    slabs = [pa([SLAB], BF16) for _ in range(NSL)]
    P_USED = pa.o

    DB = P_BYTES
    DL = P_BYTES + DYN_BYTES

    def dyn(off):
        return Bump(DB + off, DL)

    a = dyn(0)
    zs = a([NCH, DI], BF16)
    xs_tm = a([NCH, DI], BF16)
    bmT = a([NG, T], BF16)
    cmT = a([NG, T], BF16)
    bm_tm = a([NCH, 512], BF16)
    dtall = a([NCH + 1, 32], F32)
    daall = a([NCH, 32], F32)
    yznT = a([16, T], BF16)
    A_TMP = a.o - DB
    a1 = dyn(A_TMP)
    NRB = 4
    raw = [a1([515], F32) for _ in range(NRB)]
    acc = [a1([T], F32) for _ in range(NRB)]
    xsTk = [a1([T], BF16) for _ in range(NRB)]
    xstage = [a1([D], F32) for _ in range(2)]
    a2 = dyn(A_TMP)
    STf = a2([DI], F32)
    STb = a2([DI], BF16)
    yzb = [a2([DI], BF16) for _ in range(2)]
    Rg = [a2([4, 128], BF16) for _ in range(3)]
    Lx = [a2([4, 128], BF16) for _ in range(2)]
    cbm = [a2([NG, 128], BF16) for _ in range(2)]
    t1 = [a2([256], F32) for _ in range(3)]
    tD = [a2([256], F32) for _ in range(1)]
    xdtd = [a2([512], BF16) for _ in range(1)]
    Eexp = [a2([96], F32) for _ in range(2)]
    ssq = [a2([16], F32, align=64) for _ in range(2)]
    w2 = [a2([32], F32, align=64) for _ in range(2)]
    lndt = a2([NCH, 32], F32)
    Dg = a2([128], BF16)
    junkb = a2([512], BF16)
    b_ = dyn(0)
    uT = b_([KC, W], BF16)
    sg = b_([2, KC, W], BF16)
    mg = b_([KC, W], BF16)
    vln = b_([NCH, D], BF16)
    tmix = b_([8, 128], F32)
    assert b_.o - DB <= (yznT.off - DB), (b_.o - DB, yznT.off - DB)
    b2 = dyn(A_TMP)
    gv = [b2([D], F32) for _ in range(2)]
    junk2 = b2([D], BF16)
    tmpf = [b2([T], F32) for _ in range(2)]
    vstat = b2([16], F32)
    c_ = dyn(0)
    actb = c_([KFF, W], BF16)
    sgt = [c_([W], F32) for _ in range(2)]
    xb = [c_([W], BF16) for _ in range(2)]
    sqb = [c_([W], BF16) for _ in range(2)]
    msb = c_([W], F32)
    v1 = c_([W], F32)
    rstd = c_([W], F32)
    nmr = c_([W], F32)
    xn = [c_([W], F32) for _ in range(2)]
    ystage = [c_([D], F32) for _ in range(2)]
    pr = dyn(0)
    wst_f = pr([DEPTH, 8, 128], F32)
    bsb_f = pr([DEPTH, 1024], F32)
    cc_f = pr([D], F32)
    pq = pr([DEPTH, 8], F32)
    lnvg_s = pr([D], F32)
    lnvb_s = pr([D], F32)

    SBASE = DL
    PP = 4

    def smpa(off):
        return Bump(SBASE + off, ARENA)
    sp_ = smpa(0)
    aq = sp_([DEPTH, 4], F32, align=64)
    dq = sp_([DEPTH, 4], F32, align=64)
    vlnTs = sp_([KC, NSMP], F32, align=64)
    yznTs = sp_([16, NSMP], BF16, align=64)
    stmp = sp_([KC, NSMP], F32, align=64)
    stmp2 = sp_([KC, NSMP], F32, align=64)
    sgS = sp_([2, KC, NSMP], BF16, align=64)
    ocmS = sp_([KC, NSMP], BF16, align=64)
    S_DYN = (sp_.o - SBASE + 511) // 512 * 512
    s1 = smpa(S_DYN)
    rawS = s1([24, NSMP], F32)
    oldT = s1([24, 48], F32)
    xbcS = s1([24, NSMP], F32)
    accS = s1([24, NSMP], F32)
    tmpS = s1([24, NSMP], F32)
    s2 = smpa(S_DYN)
    xs_q = s2([256], F32)
    B_q = s2([128], F32)
    C_q = s2([128], F32)
    dtq = s2([16], F32)
    xdt_q = s2([256], F32)
    y_q = s2([256], F32)
    bufA = [s2([PP, 128], F32) for _ in range(3)]
    bufB = [s2([PP, 128], F32) for _ in range(2)]
    zs_q = Buf(arena, bufB[1].off, [256], F32)
    print("SBUF regions: P", P_USED, "/", P_BYTES, " A", a2.o - DB, "B", b_.o - DB, b2.o - DB, "C", c_.o - DB,
          "/", DYN_BYTES, " S", s1.o - SBASE, s2.o - SBASE, "/", SMP_BYTES)

    sems = {e: E(nc.semaphore("s_" + e)) for e in Prog.ENGS}
    dsem_names = ([f"sl{i}" for i in range(NSL)] + [f"wb{i}" for i in range(NSL)] + ["xs0", "xs1", "ys0", "ys1", "stl", "sts0", "sts1", "fin"]
                  + [f"i{i}" for i in range(12)]
                  + ["sq0", "sq1", "sq2", "sq3", "sq4", "sq5", "sa0", "sa1", "sa2", "so0", "so1", "so2", "sg0", "sg1", "sg2", "ss0", "ss1", "scv", "sd2d", "svs"])
    dsems = {n: E(nc.semaphore("d_" + n)) for n in dsem_names}

    def ks(*vs):
        out = []
        for v_ in vs:
            if v_ is not None:
                out.extend(v_.keys)
        return out

    def PS(bank, lo=0, hi=512, part=None):
        psl = slice(None) if part is None else slice(part[0], part[1])
        return V(PSA[psl, bank, lo:hi], [("ps", bank)])

    def PS2(b0, nb):
        return V(PSA[:, b0:b0 + nb, :], [("ps", b0 + i) for i in range(nb)])

    def MM(out, lhsT, rhs, start=True, stop=True, extra_r=()):
        p.add("pe", lambda e, o=out.ap, l=lhsT.ap, r=rhs.ap, s=start, t=stop:
              e.matmul(o, lhsT=l, rhs=r, start=s, stop=t, skip_group_check=True),
              reads=ks(lhsT, rhs) + list(extra_r), writes=out.keys)

    def TR(out, in_, ident):
        p.add("pe", lambda e, o=out.ap, i=in_.ap, d=ident.ap: e.transpose(o, i, d),
              reads=ks(in_, ident), writes=out.keys)

    def psum_rw(v_):
        return isinstance(v_.keys[0], tuple) and v_.keys[0][0] == "ps"

    def _rw(out, ins):
        reads = []
        writes = list(out.keys)
        for v_ in ins:
            if v_ is None:
                continue
            if psum_rw(v_):
                writes.extend(v_.keys)
            else:
                reads.extend(v_.keys)
        return reads, writes

    def ACT(out, in_, func, bias=None, scale=None, accum=None):
        reads, writes = _rw(out, [in_, bias if isinstance(bias, V) else None,
                                  scale if isinstance(scale, V) else None])
        if accum is not None:
            writes = writes + accum.keys
        kw = {}
        if bias is not None:
            kw["bias"] = bias.ap if isinstance(bias, V) else bias
        if scale is not None:
            kw["scale"] = scale.ap if isinstance(scale, V) else scale
        if accum is not None:
            kw["accum_out"] = accum.ap
        p.add("act", lambda e, o=out.ap, i=in_.ap, f=func, kw=kw: e.activation(out=o, in_=i, func=f, **kw),
              reads=reads, writes=writes)

    def sc(x):
        return x.ap if isinstance(x, V) else x

    def TS(eng, out, in0, s1, s2, op0, op1=None):
        reads, writes = _rw(out, [in0, s1 if isinstance(s1, V) else None, s2 if isinstance(s2, V) else None])
        if op1 is None:
            p.add(eng, lambda e, o=out.ap, i=in0.ap, a=sc(s1), f=op0:
                  e.tensor_scalar(out=o, in0=i, scalar1=a, scalar2=None, op0=f), reads=reads, writes=writes)
        else:
            p.add(eng, lambda e, o=out.ap, i=in0.ap, a=sc(s1), b=sc(s2), f=op0, g=op1:
                  e.tensor_scalar(out=o, in0=i, scalar1=a, scalar2=b, op0=f, op1=g), reads=reads, writes=writes)

    def TT(eng, out, in0, in1, op, in1_ap=None, in0_ap=None, out_ap=None):
        reads, writes = _rw(out, [in0, in1])
        p.add(eng, lambda e, o=(out_ap if out_ap is not None else out.ap),
              i=(in0_ap if in0_ap is not None else in0.ap),
              j=(in1_ap if in1_ap is not None else in1.ap), f=op:
              e.tensor_tensor(out=o, in0=i, in1=j, op=f), reads=reads, writes=writes)

    def STT(eng, out, in0, s, in1, op0, op1, in1_ap=None, in0_ap=None):
        reads, writes = _rw(out, [in0, in1, s if isinstance(s, V) else None])
        p.add(eng, lambda e, o=out.ap, i=(in0_ap if in0_ap is not None else in0.ap), a=sc(s),
              j=(in1_ap if in1_ap is not None else in1.ap), f=op0, g=op1:
              e.scalar_tensor_tensor(out=o, in0=i, scalar=a, in1=j, op0=f, op1=g), reads=reads, writes=writes)

    def CP(eng, out, in_, out_ap=None, in_ap=None):
        reads, writes = _rw(out, [in_])
        o = out_ap if out_ap is not None else out.ap
        i = in_ap if in_ap is not None else in_.ap
        if eng == "act":
            p.add("act", lambda e, o=o, i=i: e.copy(out=o, in_=i), reads=reads, writes=writes)
        else:
            p.add(eng, lambda e, o=o, i=i: e.tensor_copy(out=o, in_=i), reads=reads, writes=writes)

    def MEMSET(eng, out, val):
        p.add(eng, lambda e, o=out.ap, v_=val: e.memset(o, v_), writes=out.keys)

    def DMA(q, out_ap, in_ap, sem, reads=(), writes=()):
        p.add(q, lambda e, o=out_ap, i=in_ap: e.dma_start(out=o, in_=i), reads=list(reads), writes=list(writes),
              dsem=sem)

    def bc_last(v_, n):
        sh = list(v_.ap.shape)
        return v_.ap.unsqueeze(len(sh)).to_broadcast(sh + [n])

    def bc_mid(v_, n):
        sh = list(v_.ap.shape)
        return v_.ap.unsqueeze(1).to_broadcast([sh[0], n] + sh[1:])

    stream = []
    for s in range(12):
        stream.append((wada_d[0, s], None, None))
    for t in range(n_tiles):
        for l in range(DEPTH):
            if t == 0 and l == 1:
                for s in range(12):
                    stream.append((wada_d[1, s], None, None))
            for s in range(NSLAB):
                if t == 0:
                    stream.append((wstr_d[l, s], None, (l, s)) if n_tiles > 1 else (wstr_d[l, s], None, None))
                else:
                    stream.append((wcache[l, s], ("wc", l, s), None))
    wstate = {"loaded": 0, "cur": 0}

    def slab_issue():
        i = wstate["loaded"]
        if i >= len(stream):
            return
        slot = i % NSL
        src, ckey, _ = stream[i]
        DMA("pool", slabs[slot].ap, src, f"sl{slot}", reads=([ckey] if ckey else []), writes=slabs[slot].all().keys)
        wstate["loaded"] += 1

    def slab_next():
        i = wstate["cur"]
        assert i < wstate["loaded"]
        wstate["cur"] += 1
        slot = i % NSL
        wb = stream[i][2]
        if wb is not None:
            DMA("sp", wcache[wb[0], wb[1]], slabs[slot].ap, f"wb{slot}", reads=slabs[slot].all().keys,
                writes=[("wc", wb[0], wb[1])])
        return slabs[slot]

    def slab_done():
        slab_issue()

    def slab_v(sb, nk, cw, k, c0, c1):
        lo = k * cw + c0
        return V(sb.ap[:, lo:lo + (c1 - c0)], list(sb._keys(lo, lo + (c1 - c0))))

    bankctr = {"n": 0}

    def nextbank():
        b = bankctr["n"] % 4
        bankctr["n"] += 1
        return b

    for _ in range(NSL):
        slab_issue()
    DMA("sp", cf32.ap, consts_d.rearrange("p (a b) -> p a b", a=6), "i0", writes=cf32.all().keys)
    DMA("sp", pcol.ap, pcol_d.rearrange("p (a b) -> p a b", a=DEPTH), "i1", writes=pcol.all().keys)
    DMA("sp", prow.ap, prow_d.rearrange("p (a b) -> p a b", a=DEPTH), "i2", writes=prow.all().keys)
    DMA("sp", cc_f.ap[0:17, :], cc_d, "i3", writes=cc_f.all().keys)
    DMA("sp", wst_f.ap, wst_d.rearrange("p (a b c) -> p a b c", a=DEPTH, b=8), "i4", writes=wst_f.all().keys)
    DMA("sp", bsb_f.ap, bsb_d.rearrange("p (a b) -> p a b", a=DEPTH), "i5", writes=bsb_f.all().keys)
    DMA("pool", wdt.ap, wdt_d.rearrange("p (a b c) -> p a b c", a=DEPTH, b=KC), "i6", writes=wdt.all().keys)
    DMA("sp", pq.ap, pq_d.rearrange("p (a b) -> p a b", a=DEPTH), "i7", writes=pq.all().keys)

    ident = cf32.v(0, None)
    triu = cf32.v(1, None)
    trilS = cf32.v(2, None)
    ones = cf32.v(3, None)
    CP("dve", cb16.all(), cf32.v((0, 4), None))
    identb = cb16.v(0, None)
    triub = cb16.v(1, None)
    trilSb = cb16.v(2, None)
    MEMSET("dve", cM.all(), 1.0 / 1024.0)
    MEMSET("dve", hist.all(), 0.0)
    ACT(abc.all(), prow.v(None, (32, 64)), AF.Exp)
    TS("dve", abc.all(), abc.all(), -1.0, None, ALU.mult)
    ACT(aq.all(), pq.v(None, (0, 4)), AF.Exp)
    TS("dve", aq.all(), aq.all(), -1.0, None, ALU.mult)
    CP("dve", dq.all(), pq.v(None, (4, 8)))
    TT("dve", wst_f.all(), wst_f.all(), triu, ALU.mult,
       in1_ap=triu.ap.unsqueeze(1).unsqueeze(1).to_broadcast([128, DEPTH, 8, 128]))
    CP("act", wsTb.all(), wst_f.all())
    for l in range(DEPTH):
        for hf in range(2):
            MM(PS(hf), ones, V(wst_f.ap[:, l, :, :].rearrange("p a b -> p (a b)")[:, hf * 512:hf * 512 + 512], wst_f.v(l, (4 * hf, 4 * hf + 4), None).keys))
        for g in range(8):
            hf, gg = divmod(g, 4)
            STT("dve", BIASb.v(l, g, None), PS(hf, gg * 128, gg * 128 + 128), pcol.v(l, (LNVB + g, LNVB + g + 1)),
                bsb_f.v(l, (g * 128, g * 128 + 128)), ALU.mult, ALU.add)
    ACT(V(cc_f.ap[0:17, :], cc_f.all().keys), V(cc_f.ap[0:17, :], cc_f.all().keys), AF.Silu)
    for k in range(KC):
        TR(PS(4, k * 17, k * 17 + 17), V(cc_f.ap[0:17, k * 128:(k + 1) * 128], cc_f.all().keys),
           V(cf32.ap[0:17, 0, 0:17], ident.keys))
    CP("dve", csT.all(), PS(4, 0, KC * 17), in_ap=PSA[:, 4, 0:KC * 17].rearrange("p (a b) -> p a b", a=KC))
    def compute_mod(l):
        for s in range(12):
            sb = slab_next()
            for jj in range(4):
                j = 4 * s + jj
                b = nextbank()
                for k in range(KC):
                    MM(PS(b, 0, 17), slab_v(sb, KC, 512, k, jj * 128, jj * 128 + 128), csT.v(k, None),
                       start=(k == 0), stop=(k == KC - 1))
                TS("dve", mod.v(l, j, None), PS(b, 0, 17), pcol.v(l, (BADA + j, BADA + j + 1)), None, ALU.add)
            slab_done()
        TS("dve", mod.v(l, (8, 16), None), mod.v(l, (8, 16), None), 1.0, None, ALU.add)
        TS("dve", mod.v(l, (32, 40), None), mod.v(l, (32, 40), None), 1.0, None, ALU.add)
        TS("dve", mod.v(l, (16, 24), None), mod.v(l, (16, 24), None), 1.0 / ALPHA, None, ALU.mult)
        TS("dve", mod.v(l, (40, 48), None), mod.v(l, (40, 48), None), 1.0 / ALPHA, None, ALU.mult)
        TT("dve", mod2.v(l, (0, 8), None), mod.v(l, (32, 40), None), pcol.v(l, (LN1G, LN1G + 8)), ALU.mult,
           in1_ap=bc_last(pcol.v(l, (LN1G, LN1G + 8)), 17))
        TT("dve", mod2.v(l, (8, 16), None), mod.v(l, (32, 40), None), pcol.v(l, (LN1B, LN1B + 8)), ALU.mult,
           in1_ap=bc_last(pcol.v(l, (LN1B, LN1B + 8)), 17))
        TT("dve", mod2.v(l, (8, 16), None), mod2.v(l, (8, 16), None), mod.v(l, (24, 32), None), ALU.add)
    compute_mod(0)
    MEMSET("dve", STf.all(), 0.0)
    for l in range(DEPTH):
        DMA("sp", scrST[l], STf.ap, f"sts{l}", reads=STf.all().keys, writes=[("scrST", l)])

    def modc(l, j, col=0):
        return mod.v(l, j, (col, col + 1))

    def layer_norm_apply(l, which, smp):
        gcol = LN1G if which == 1 else LN2G
        bcol = LN1B if which == 1 else LN2B
        for j in range(KC):
            q = j % 2
            ACT(xb[q].v((0, T)), xT.v(j, (0, T)), AF.Copy)
            ACT(sqb[q].v((0, T)), xT.v(j, (0, T)), AF.Square)
            MM(PS(6), cMm, xb[q].v((0, T)), start=(j == 0), stop=(j == KC - 1))
            MM(PS(7), cMm, sqb[q].v((0, T)), start=(j == 0), stop=(j == KC - 1))
        CP("act", msb.v((0, T)), PS(6))
        TT("dve", v1.v((0, T)), msb.v((0, T)), msb.v((0, T)), ALU.mult)
        TT("dve", v1.v((0, T)), PS(7), v1.v((0, T)), ALU.subtract)
        ACT(v1.v((0, T)), v1.v((0, T)), AF.Sqrt, bias=epsln_c, scale=1.0)
        p.add("dve", lambda e, o=rstd.ap[:, 0:T], i=v1.ap[:, 0:T]: e.reciprocal(out=o, in_=i),
              reads=v1.v((0, T)).keys, writes=rstd.v((0, T)).keys)
        STT("dve", nmr.v((0, T)), msb.v((0, T)), -1.0, rstd.v((0, T)), ALU.mult, ALU.mult)
        for j in range(KC):
            q = j % 2
            TT("dve", xn[q].v((0, T)), xT.v(j, (0, T)), rstd.v((0, T)), ALU.mult)
            TT("pool", xn[q].v((0, T)), xn[q].v((0, T)), nmr.v((0, T)), ALU.add)
            ACT(xT.v(j, (0, T)), xn[q].v((0, T)), AF.Identity, bias=pcol.v(l, (bcol + j, bcol + j + 1)),
                scale=pcol.v(l, (gcol + j, gcol + j + 1)))
            if which == 1:
                TS("dve", hT.v(j, (0, T)), xn[q].v((0, T)), mod2.v(l, j, (0, 1)), mod2.v(l, 8 + j, (0, 1)),
                   ALU.mult, ALU.add)
        if smp:
            for j in range(KC):
                q = j % 2
                ACT(xb[q].v(SC), xT.v(j, SC), AF.Copy)
                ACT(sqb[q].v(SC), xT.v(j, SC), AF.Square)
                MM(PS(4, 0, NSMP), cMm, xb[q].v(SC), start=(j == 0), stop=(j == KC - 1))
                MM(PS(5, 0, NSMP), cMm, sqb[q].v(SC), start=(j == 0), stop=(j == KC - 1))
            CP("act", msb.v(SC), PS(4, 0, NSMP))
            TT("dve", v1.v(SC), msb.v(SC), msb.v(SC), ALU.mult)
            TT("dve", v1.v(SC), PS(5, 0, NSMP), v1.v(SC), ALU.subtract)
            ACT(v1.v(SC), v1.v(SC), AF.Sqrt, bias=epsln_c, scale=1.0)
            p.add("dve", lambda e, o=rstd.ap[:, T:W], i=v1.ap[:, T:W]: e.reciprocal(out=o, in_=i),
                  reads=v1.v(SC).keys, writes=rstd.v(SC).keys)
            STT("dve", nmr.v(SC), msb.v(SC), -1.0, rstd.v(SC), ALU.mult, ALU.mult)
            TT("dve", stmp2.all(), xT.v(None, SC), rstd.v(SC), ALU.mult, in1_ap=bc_mid(rstd.v(SC), KC))
            TT("dve", stmp2.all(), stmp2.all(), nmr.v(SC), ALU.add, in1_ap=bc_mid(nmr.v(SC), KC))
            TT("dve", stmp.all(), stmp2.all(), pcol.v(l, (gcol, gcol + 8)), ALU.mult,
               in1_ap=bc_last(pcol.v(l, (gcol, gcol + 8)), NSMP))
            TT("dve", xT.v(None, SC), stmp.all(), pcol.v(l, (bcol, bcol + 8)), ALU.add,
               in1_ap=bc_last(pcol.v(l, (bcol, bcol + 8)), NSMP))
            if which == 1:
                TT("dve", stmp.all(), stmp2.all(), mod2.v(l, (0, 8), (1, 17)), ALU.mult)
                TT("dve", hT.v(None, SC), stmp.all(), mod2.v(l, (8, 16), (1, 17)), ALU.add)

    cMm = cM.all()
    epsln_c = smalls.v((0, 1))
    eps_c = smalls.v((1, 2))
    MEMSET("dve", smalls.v((0, 1)), EPS_LN)
    MEMSET("dve", smalls.v((1, 2)), EPS)

    sbctr = {"n": 0}

    def sbank():
        b = 4 + (sbctr["n"] % 2)
        sbctr["n"] += 1
        return b

    SC = (T, W)

    def stage16(q):
        return V(xstage[q].ap[0:16, 0:512], xstage[q].v((0, 512)).keys)

    def smp_ride_fm(sb, nk, cw, c0, rhs_fn, evac):
        b = sbank()
        for k in range(nk):
            MM(PS(b, 0, NSMP), slab_v(sb, nk, cw, k, c0, c0 + 128), rhs_fn(k), start=(k == 0), stop=(k == nk - 1))
        evac(PS(b, 0, NSMP))

    def smp_ride_tm(sb, evac):
        b = sbank()
        for k in range(KC):
            MM(PS(b, 0, 512, part=(0, NSMP)), hT.v(k, SC), slab_v(sb, KC, 512, k, 0, 512),
               start=(k == 0), stop=(k == KC - 1))
        evac(PS(b, 0, 512, part=(0, NSMP)))

    def smp_conv(l):
        for pc in range(6):
            q = pc % 2
            st_v = V(xstage[q].ap[0:48, 0:512], xstage[q].v((0, 512)).keys)
            DMA("sp", st_v.ap, sconv_d[l][:, pc * 512:(pc + 1) * 512], f"xs{q}", writes=st_v.keys)
            b = sbank()
            for kk in range(4):
                TR(PS(b, kk * 48, kk * 48 + 48), V(xstage[q].ap[0:48, kk * 128:(kk + 1) * 128], st_v.keys),
                   V(cf32.ap[0:48, 0, 0:48], ident.keys))
            CP("dve", oldT.v((4 * pc, 4 * pc + 4), None), PS(b, 0, 192),
               in_ap=PSA[:, b, 0:192].rearrange("p (a b) -> p a b", a=4))
        DMA("sp", convs_d[l][:, 0:2, :], sconv_d[l].rearrange("(b k) c -> b k c", k=3)[:, 1:3, :], "sd2d")
        old4 = oldT.ap.rearrange("p j (b k) -> p j b k", k=3)
        wk = lambda kk: bc_last(pcol.v(l, (CONVW + kk * 24, CONVW + kk * 24 + 24)), NSMP)
        TT("dve", accS.all(), oldT.all(), pcol.v(l, (CONVW, CONVW + 24)), ALU.mult, in0_ap=old4[:, :, :, 0], in1_ap=wk(0))
        for kk in (1, 2):
            TT("dve", tmpS.all(), oldT.all(), pcol.v(l, (CONVW + kk * 24, CONVW + kk * 24 + 24)), ALU.mult,
               in0_ap=old4[:, :, :, kk], in1_ap=wk(kk))
            TT("dve", accS.all(), accS.all(), tmpS.all(), ALU.add)
        TT("dve", tmpS.all(), rawS.all(), pcol.v(l, (CONVW + 72, CONVW + 96)), ALU.mult, in1_ap=wk(3))
        TT("dve", accS.all(), accS.all(), tmpS.all(), ALU.add)
        TT("dve", accS.all(), accS.all(), pcol.v(l, (CONVB, CONVB + 24)), ALU.add,
           in1_ap=bc_last(pcol.v(l, (CONVB, CONVB + 24)), NSMP))
        ACT(xbcS.all(), accS.all(), AF.Silu)
        for pc in range(6):
            q = pc % 2
            b = sbank()
            for kk in range(4):
                j = 4 * pc + kk
                TR(PS(b, kk * 128, kk * 128 + 128, part=(0, NSMP)), xbcS.v(j, None), ident)
            CP("act", stage16(q), PS(b, 0, 512, part=(0, NSMP)))
            if pc < 4:
                DMA("sp", scrX[l][:, pc * 512:(pc + 1) * 512], stage16(q).ap, f"ss{q}", reads=stage16(q).keys,
                    writes=[("scrX", l)])
            else:
                dst = scrB if pc == 4 else scrC
                for dup in range(2):
                    DMA("sp", dst[l].rearrange("b (g d n) -> b g d n", g=NG, d=2)[:, :, dup, :],
                        stage16(q).ap.rearrange("b (g n) -> b g n", g=NG), f"ss{q}", reads=stage16(q).keys,
                        writes=[("scrBC", l, pc)])

    def smp_ssd(l):
        DMA("sp", xs_q.ap, scrX[l].rearrange("b (q c) -> (b q) c", q=8), "sq0", reads=[("scrX", l)], writes=xs_q.all().keys)
        DMA("sp", B_q.ap, scrB[l].rearrange("b (q n) -> (b q) n", q=8), "sq2", reads=[("scrBC", l, 4)], writes=B_q.all().keys)
        DMA("sp", C_q.ap, scrC[l].rearrange("b (q n) -> (b q) n", q=8), "sq3", reads=[("scrBC", l, 5)], writes=C_q.all().keys)
        DMA("sp", dtq.ap[:, 0:4], scrD[l].rearrange("b (q i) -> (b q) i", q=8), "sq4", reads=[("scrD", l)], writes=dtq.all().keys)
        TT("dve", dtq.v((4, 8)), dtq.v((0, 4)), aq.v(l, None), ALU.mult)
        ACT(dtq.v((4, 8)), dtq.v((4, 8)), AF.Exp)
        TT("dve", xdt_q.all(), xs_q.all(), dtq.v((0, 4)), ALU.mult,
           in0_ap=xs_q.ap.rearrange("p (i c) -> p i c", i=4), in1_ap=bc_last(dtq.v((0, 4)), HP),
           out_ap=xdt_q.ap.rearrange("p (i c) -> p i c", i=4))
        pieces = [(hi, pp) for hi in range(4) for pp in range(HP // PP)]

        def rng(n):
            hi, pp = pieces[n]
            lo = (pp * PP) * NST
            return hi, pp, lo, lo + PP * NST

        def load(n):
            hi, pp, lo, hi_ = rng(n)
            A = bufA[n % 3]
            DMA("sp", A.ap.rearrange("p a n -> p (a n)"), sssd_d[l][:, hi, lo:hi_], f"sa{n % 3}", writes=A.all().keys)

        load(0)
        load(1)
        for n in range(len(pieces)):
            hi, pp, lo, hi_ = rng(n)
            if n + 2 < len(pieces):
                load(n + 2)
            A = bufA[n % 3]
            Bf = bufB[n % 2]
            c0 = hi * HP + pp * PP
            for i4 in range(PP):
                ACT(Bf.v(i4, None), B_q.all(), AF.Copy, scale=V(xdt_q.ap[:, c0 + i4:c0 + i4 + 1], xdt_q.all().keys))
            STT("dve", A.all(), A.all(), dtq.v((4 + hi, 5 + hi)), Bf.all(), ALU.mult, ALU.add)
            DMA("sp", ssds_d[l][:, hi, lo:hi_], A.ap.rearrange("p a n -> p (a n)"), f"so{n % 3}", reads=A.all().keys)
            TT("pool", Bf.all(), A.all(), C_q.all(), ALU.mult, in1_ap=bc_mid(C_q.all(), PP))
            p.add("dve", lambda e, o=y_q.ap[:, c0:c0 + PP], i=Bf.ap:
                  e.tensor_reduce(out=o, in_=i, axis=AX.X, op=ALU.add), reads=Bf.all().keys, writes=y_q.all().keys)
            yield
        DMA("sp", zs_q.ap, scrZ[l].rearrange("b (q c) -> (b q) c", q=8), "sq1", reads=[("scrZ", l)], writes=zs_q.all().keys)
        TT("dve", xdt_q.all(), xs_q.all(), dq.v(l, None), ALU.mult,
           in0_ap=xs_q.ap.rearrange("p (i c) -> p i c", i=4), in1_ap=bc_last(dq.v(l, None), HP),
           out_ap=xdt_q.ap.rearrange("p (i c) -> p i c", i=4))
        TT("dve", y_q.all(), y_q.all(), xdt_q.all(), ALU.add)
        TT("dve", y_q.all(), y_q.all(), zs_q.all(), ALU.mult)
        MEMSET("dve", dtq.v((8, 9)), 0.0)
        ACT(xdt_q.all(), y_q.all(), AF.Square, accum=dtq.v((8, 9)))
        MM(PS(6, 0, 1), cf32.v(4, None), dtq.v((8, 9)))
        ACT(dtq.v((10, 11)), PS(6, 0, 1), AF.Sqrt, bias=eps_c, scale=1.0 / DI)
        p.add("dve", lambda e, o=dtq.ap[:, 11:12], i=dtq.ap[:, 10:11]: e.reciprocal(out=o, in_=i),
              reads=dtq.all().keys, writes=dtq.all().keys)
        TS("dve", y_q.all(), y_q.all(), dtq.v((11, 12)), None, ALU.mult)
        for kc in range(16):
            hq_, hh = divmod(kc, 2)
            MM(PS(7, kc * NSMP, kc * NSMP + NSMP), y_q.v((hh * 128, hh * 128 + 128)),
               V(cf32.ap[:, 5, hq_ * NSMP:(hq_ + 1) * NSMP], cf32.v(5, None).keys))
        TT("dve", yznTs.all(), PS(7, 0, 256), pcol.v(l, (NORMW, NORMW + 16)), ALU.mult,
           in0_ap=PSA[:, 7, 0:256].rearrange("p (a b) -> p a b", a=16),
           in1_ap=bc_last(pcol.v(l, (NORMW, NORMW + 16)), NSMP))
        yield

    def smp_advance(gen, n):
        if gen is None:
            return
        for _ in range(n):
            try:
                next(gen)
            except StopIteration:
                return

    if n_tiles >= 4:
        IN_AT = {0: (1, 0), 1: (2, 1)}
        OUT_AT = {0: (2, 0), 1: (3, 1)}
    else:
        IN_AT = {0: (0, 0), 1: (0, 1)}
        OUT_AT = {0: (0, 0), 1: (0, 1)}
    G = {"gen": None}

    def tile_layer(t, l):
        last_tile = (t == n_tiles - 1)
        smp_in = with_samples and IN_AT[l] == (t, l)
        smp_out = with_samples and OUT_AT[l] == (t, l)
        def tick(n=1):
            if n > 1:
                return smp_advance(G["gen"], n)
            G["ctr"] = G.get("ctr", 0) + 1
            if G["ctr"] % 2 == 0:
                smp_advance(G["gen"], 1)
        tick_lo = (lambda n=1: None) if smp_in else tick
        p.tag = f"t{t}l{l}:in"
        if l == 0 and smp_in:
            DMA("sp", xstage[0].ap[0:NSMP, :], xs_d, "xs0", writes=xstage[0].all().keys)
            for k in range(KC):
                TR(PS(4, k * NSMP, k * NSMP + NSMP), V(xstage[0].ap[0:NSMP, k * 128:(k + 1) * 128], xstage[0].all().keys),
                   V(cf32.ap[0:NSMP, 0, 0:NSMP], ident.keys))
            CP("dve", xT.v(None, SC), PS(4, 0, KC * NSMP), in_ap=PSA[:, 4, 0:KC * NSMP].rearrange("p (a b) -> p a b", a=KC))
        if l == 0:
            for c in range(NCH):
                q = c % 2
                r0 = t * T + c * 128
                DMA("sp", xstage[q].ap, xp_d[r0:r0 + 128, :], f"xs{q}", writes=xstage[q].all().keys)
                for hf in range(2):
                    for kk in range(4):
                        k = hf * 4 + kk
                        TR(PS(4 + hf, kk * 128, kk * 128 + 128), xstage[q].v((k * 128, k * 128 + 128)), ident)
                    CP("act" if hf == 0 else "dve", xT.v((hf * 4, hf * 4 + 4), (c * 128, c * 128 + 128)), PS(4 + hf),
                       in_ap=PSA[:, 4 + hf, :].rearrange("p (a b) -> p a b", a=4))
        p.tag = f"t{t}l{l}:ph0"
        for j in range(KC):
            TS("dve" if j % 2 == 0 else "pool", hT.v(j, (0, T)), xT.v(j, (0, T)), modc(l, 8 + j), modc(l, j),
               ALU.mult, ALU.add)
        if smp_in:
            TT("dve", stmp.all(), xT.v(None, SC), mod.v(l, (8, 16), (1, 17)), ALU.mult)
            TT("dve", hT.v(None, SC), stmp.all(), mod.v(l, (0, 8), (1, 17)), ALU.add)
        p.tag = f"t{t}l{l}:ph1"
        for c in range(NCH):
            for k in range(KC):
                MM(PS(4, c * 32, c * 32 + 32), hT.v(k, (c * 128, c * 128 + 128)), wdt.v(l, k, None),
                   start=(k == 0), stop=(k == KC - 1))
        dtv = V(dtall.ap[:, 0:NCH, :], dtall.all().keys)
        TT("dve", dtv, PS(4, 0, 128), prow.v(l, (0, 32)), ALU.add,
           in0_ap=PSA[:, 4, 0:128].rearrange("p (a b) -> p a b", a=NCH), in1_ap=bc_mid(prow.v(l, (0, 32)), NCH))
        dav = daall.all()
        STT("dve", dav, dtv, -1.0, dtv, ALU.mult, ALU.max)
        ACT(dav, dav, AF.Exp, scale=-1.0)
        ACT(dav, dav, AF.Ln, bias=1.0)
        STT("dve", dtv, dtv, 0.0, dav, ALU.max, ALU.add)
        TT("dve", dav, dtv, abc.v(l, None), ALU.mult, in1_ap=bc_mid(abc.v(l, None), NCH))
        if smp_in:
            for k in range(KC):
                MM(PS(5, 0, 32, part=(0, NSMP)), hT.v(k, SC), wdt.v(l, k, None), start=(k == 0), stop=(k == KC - 1))
            dts = V(dtall.ap[0:NSMP, NCH, :], dtall.all().keys)
            dts2 = V(stmp2.ap[0:NSMP, 0:2, :].rearrange("p a b -> p (a b)"), stmp2.all().keys)
            TT("dve", dts, PS(5, 0, 32, part=(0, NSMP)), V(prow.ap[0:NSMP, l, 0:32], prow.v(l, (0, 32)).keys), ALU.add)
            STT("dve", dts2, dts, -1.0, dts, ALU.mult, ALU.max)
            ACT(dts2, dts2, AF.Exp, scale=-1.0)
            ACT(dts2, dts2, AF.Ln, bias=1.0)
            STT("dve", dts, dts, 0.0, dts2, ALU.max, ALU.add)
            DMA("sp", scrD[l], dts.ap, "sg0", reads=dts.keys, writes=[("scrD", l)])
        pend = []
        pendS = []

        def xbc_slab(s):
            sb = slab_next()
            for jj in range(4):
                j = 4 * s + jj
                b = nextbank()
                for k in range(KC):
                    MM(PS(b), slab_v(sb, KC, 512, k, jj * 128, jj * 128 + 128), hT.v(k, (0, T)),
                       start=(k == 0), stop=(k == KC - 1))
                if smp_in:
                    smp_ride_fm(sb, KC, 512, jj * 128, lambda k: hT.v(k, SC),
                                lambda pv, j=j: CP("act", rawS.v(j, None), pv))
                q = j % NRB
                CP("dve", raw[q].v((0, 3)), hist.v(l, j, None))
                CP("act", raw[q].v((3, 515)), PS(b))
                cw = lambda kk, l=l, j=j: pcol.v(l, (CONVW + kk * 24 + j, CONVW + kk * 24 + j + 1))
                ACT(acc[q].all(), PS(b), AF.Copy, scale=cw(3))
                for kk in range(3):
                    STT("dve", acc[q].all(), raw[q].v((kk, kk + T)), cw(kk), acc[q].all(), ALU.mult, ALU.add)
                CP("pool", hist.v(l, j, None), raw[q].v((512, 515)))
                if j < 16:
                    dst = xsTk[q].all()
                elif j < 20:
                    dst = bmT.v(j - 16, None)
                else:
                    dst = cmT.v(j - 20, None)
                pendS.append(lambda dst=dst, q=q, j=j: ACT(dst, acc[q].all(), AF.Silu,
                                                          bias=pcol.v(l, (CONVB + j, CONVB + j + 1))))
                if len(pendS) > 1:
                    pendS.pop(0)()
                tick()
                if j < 20:
                    def do_tr(j=j, dst=dst):
                        pb = 6 + (j % 2)
                        psb = V(PSA[:, pb, 0:256].bitcast(BF16), [("ps", pb)])
                        for c in range(NCH):
                            TR(V(psb.ap[:, c * 128:(c + 1) * 128], psb.keys),
                               V(dst.ap[:, c * 128:(c + 1) * 128], dst.keys), identb)
                        if j < 16:
                            o = xs_tm.v(None, (j * 128, j * 128 + 128))
                        else:
                            o = bm_tm.v(None, ((j - 16) * 128, (j - 16) * 128 + 128))
                        CP("dve", o, psb, in_ap=psb.ap.rearrange("p (a b) -> p a b", a=NCH))
                    pend.append(do_tr)
                    if len(pend) > 4:
                        pend.pop(0)()
            if smp_in:
                def ev_raw(pv, s=s):
                    CP("act", stage16(s % 2), pv)
                    DMA("sp", convs_d[l][:, 2, s * 512:(s + 1) * 512], stage16(s % 2).ap, f"ss{s % 2}",
                        reads=stage16(s % 2).keys)
                smp_ride_tm(sb, ev_raw)
            slab_done()
        def z_slab(s):
            sb = slab_next()
            for c in range(NCH):
                b = nextbank()
                for k in range(KC):
                    MM(PS(b), hT.v(k, (c * 128, c * 128 + 128)), slab_v(sb, KC, 512, k, 0, 512),
                       start=(k == 0), stop=(k == KC - 1))
                ACT(zs.v(c, (s * 512, s * 512 + 512)), PS(b), AF.Silu)
            if smp_in:
                def ev_z(pv, s=s):
                    ACT(stage16(s % 2), pv, AF.Silu)
                    DMA("sp", scrZ[l][:, s * 512:(s + 1) * 512], stage16(s % 2).ap, f"ss{s % 2}",
                        reads=stage16(s % 2).keys, writes=[("scrZ", l)])
                smp_ride_tm(sb, ev_z)
            slab_done()
        for s in range(4):
            xbc_slab(s)
            z_slab(s)
        xbc_slab(4)
        xbc_slab(5)
        while pendS:
            pendS.pop(0)()
        while pend:
            pend.pop(0)()
        if smp_in:
            smp_conv(l)
            G["gen"] = smp_ssd(l)
        p.tag = f"t{t}l{l}:ssd"
        DMA("sp", STf.ap, scrST[l], "stl", reads=[("scrST", l)], writes=STf.all().keys)
        CP("act", STb.all(), STf.all())
        ACT(lndt.all(), dtv, AF.Ln)
        items = [(c, hg) for c in range(NCH) for hg in range(8)]

        def chunk_prep(c):
            cp_ = c % 2
            tok = (c * 128, c * 128 + 128)
            da = daall.v(c, None)
            MM(PS(5, 0, 32), triu, da)
            MM(PS(5, 32, 64), trilS, da)
            MM(PS(5, 64, 96), ones, da)
            ACT(Eexp[cp_].all(), PS(5, 0, 96), AF.Exp)
            TT("dve", w2[cp_].all(), dtall.v(c, None), Eexp[cp_].v((32, 64)), ALU.mult)
            for g in range(NG):
                MM(PS(6, g * 128, g * 128 + 128), bmT.v(g, tok), cmT.v(g, tok))
            TT("dve", cbm[cp_].all(), PS(6), triu, ALU.mult,
               in0_ap=PSA[:, 6, :].rearrange("p (a b) -> p a b", a=NG), in1_ap=bc_mid(triu, NG))

        def prepR(i):
            c, hg = items[i]
            h0 = 4 * hg
            TT("pool", Rg[i % 3].all(), triu, daall.v(c, None), ALU.mult, in0_ap=bc_mid(triu, 4),
               in1_ap=bc_last(V(daall.ap[:, c, h0:h0 + 4], daall.v(c, None).keys), 128))

        def prep(i):
            c, hg = items[i]
            par = i % 2
            g = hg // 2
            h0 = 4 * hg
            MM(PS(par), trilSb, V(Rg[i % 3].ap.rearrange("p a b -> p (a b)"), Rg[i % 3].all().keys))
            for e4 in range(4):
                ACT(Lx[par].v(e4, None), PS(par, e4 * 128, e4 * 128 + 128), AF.Exp,
                    bias=V(lndt.ap[:, c, h0 + e4:h0 + e4 + 1], lndt.v(c, None).keys))
            TT("dve", Lx[par].all(), Lx[par].all(), cbm[c % 2].v(g, None), ALU.mult, in1_ap=bc_mid(cbm[c % 2].v(g, None), 4))

        def main(i):
            c, hg = items[i]
            par = i % 2
            cp_ = c % 2
            g = hg // 2
            h0 = 4 * hg
            tok = (c * 128, c * 128 + 128)
            cols = (hg * 256, hg * 256 + 256)
            yb = 2 + par
            for e4 in range(4):
                h = h0 + e4
                MM(PS(yb, e4 * 64, e4 * 64 + 64), Lx[par].v(e4, None), xs_tm.v(c, (h * HP, h * HP + HP)))
            MM(PS(yb, 256, 512), cmT.v(g, tok), STb.v(cols))
            v4 = lambda ap: ap.rearrange("p (a b) -> p a b", a=4)
            TT("dve", t1[i % 3].all(), PS(yb, 256, 512), Eexp[cp_].v((h0, h0 + 4)), ALU.mult,
               in0_ap=v4(PSA[:, yb, 256:512]), in1_ap=bc_last(Eexp[cp_].v((h0, h0 + 4)), HP), out_ap=v4(t1[i % 3].ap))
            TT("dve", tD[0].all(), xs_tm.v(c, cols), prow.v(l, (64 + h0, 64 + h0 + 4)), ALU.mult,
               in0_ap=v4(xs_tm.ap[:, c, cols[0]:cols[1]]), in1_ap=bc_last(prow.v(l, (64 + h0, 64 + h0 + 4)), HP),
               out_ap=v4(tD[0].ap))
            TT("dve", t1[i % 3].all(), PS(yb, 0, 256), t1[i % 3].all(), ALU.add)
            TT("dve", t1[i % 3].all(), t1[i % 3].all(), tD[0].all(), ALU.add)
            if i > 0:
                gate(i - 1)

        def gate(i):
            c, hg = items[i]
            cols = (hg * 256, hg * 256 + 256)
            TT("pool", yzb[c % 2].v(cols), t1[i % 3].all(), zs.v(c, cols), ALU.mult)

        v8 = lambda ap: ap.rearrange("p (a b) -> p a b", a=8)

        def su1(c, g):
            cp_ = c % 2
            gc = (g * 512, g * 512 + 512)
            hs = (8 * g, 8 * g + 8)
            TT("pool", xdtd[0].all(), xs_tm.v(c, gc), w2[cp_].v(hs), ALU.mult,
               in0_ap=v8(xs_tm.ap[:, c, gc[0]:gc[1]]), in1_ap=bc_last(w2[cp_].v(hs), HP), out_ap=v8(xdtd[0].ap))
            TT("pool", STf.v(gc), STf.v(gc), Eexp[cp_].v((64 + hs[0], 64 + hs[1])), ALU.mult,
               in0_ap=v8(STf.ap[:, gc[0]:gc[1]]), in1_ap=bc_last(Eexp[cp_].v((64 + hs[0], 64 + hs[1])), HP),
               out_ap=v8(STf.ap[:, gc[0]:gc[1]]))

        def su2(c, g):
            cp_ = c % 2
            gc = (g * 512, g * 512 + 512)
            hs = (8 * g, 8 * g + 8)
            MM(PS(4), bm_tm.v(c, (g * 128, g * 128 + 128)), xdtd[0].all())
            TT("dve", STf.v(gc), PS(4), STf.v(gc), ALU.add)

        def su3(c, g):
            gc = (g * 512, g * 512 + 512)
            CP("act", STb.v(gc), STf.v(gc))

        def ce_steps(c):
            cp_ = c % 2
            sq_ = ssq[cp_]
            tok = (c * 128, c * 128 + 128)
            steps = []

            def sq(g):
                if g == 0:
                    MEMSET("dve", sq_.all(), 0.0)
                ACT(junkb.all(), yzb[cp_].v((g * 512, g * 512 + 512)), AF.Square, accum=sq_.v((g, g + 1)))

            def stats():
                p.add("dve", lambda e, o=sq_.ap[:, 8:9], i=sq_.ap[:, 0:8]: e.tensor_reduce(out=o, in_=i, axis=AX.X, op=ALU.add),
                      reads=sq_.all().keys, writes=sq_.all().keys)
                ACT(sq_.v((9, 10)), sq_.v((8, 9)), AF.Sqrt, bias=eps_c, scale=1.0 / DI)
                p.add("dve", lambda e, o=sq_.ap[:, 10:11], i=sq_.ap[:, 9:10]: e.reciprocal(out=o, in_=i),
                      reads=sq_.all().keys, writes=sq_.all().keys)
                TS("dve", Dg.all(), identb, sq_.v((10, 11)), None, ALU.mult)

            def nt(q4):
                nb = 7 - (q4 % 2)
                for kk in range(4):
                    kc = 4 * q4 + kk
                    MM(PS(nb, kk * 128, kk * 128 + 128), yzb[cp_].v((kc * 128, kc * 128 + 128)), Dg.all())
                TT("dve", yznT.v((4 * q4, 4 * q4 + 4), tok), PS(nb), pcol.v(l, (NORMW + 4 * q4, NORMW + 4 * q4 + 4)), ALU.mult,
                   in0_ap=PSA[:, nb, :].rearrange("p (a b) -> p a b", a=4),
                   in1_ap=bc_last(pcol.v(l, (NORMW + 4 * q4, NORMW + 4 * q4 + 4)), 128))
            for g in range(NG):
                steps.append(lambda g=g: sq(g))
            steps.append(stats)
            for q4 in range(4):
                steps.append(lambda q4=q4: nt(q4))
            return steps

        todo = {}
        for c in range(NCH):
            for g in range(NG):
                n0 = c * 8 + 2 * g + 1
                todo.setdefault(n0 + 2, []).append(lambda c=c, g=g: su2(c, g))
                todo.setdefault(n0, []).append(lambda c=c, g=g: su1(c, g))
                todo.setdefault(n0 + 3, []).append(lambda c=c, g=g: su3(c, g))
        chunk_prep(0)
        prepR(0)
        prepR(1)
        prep(0)
        for i in range(len(items)):
            c, hg = items[i]
            if hg == 2 and c + 1 < NCH:
                chunk_prep(c + 1)
            if i + 2 < len(items):
                prepR(i + 2)
            if i + 1 < len(items):
                prep(i + 1)
            main(i)
            for fn in todo.pop(i, []):
                fn()
            if c > 0:
                if hg == 0:
                    cesteps = ce_steps(c - 1)
                cesteps.pop(0)()
                if hg == 7:
                    cesteps.pop(0)()
        gate(len(items) - 1)
        for i in sorted(todo):
            for fn in todo[i]:
                fn()
        for fn in ce_steps(NCH - 1):
            fn()
        DMA("sp", scrST[l], STf.ap, f"sts{l}", reads=STf.all().keys, writes=[("scrST", l)])
        if last_tile:
            for q4 in range(4):
                for kk in range(4):
                    kc = 4 * q4 + kk
                    TR(PS(4 + (q4 % 2), kk * 128, kk * 128 + 128), STf.v((kc * 128, kc * 128 + 128)), ident)
                yq = ystage_a[q4 % 2]
                CP("act", yq.v((0, 512)), PS(4 + (q4 % 2)))
                DMA("sp", ssdp_d[l].rearrange("(a p) n -> p a n", p=128)[:, 4 * q4:4 * q4 + 4, :],
                    yq.ap[:, 0:512].rearrange("p (a n) -> p a n", a=4), f"ys{q4 % 2}", reads=yq.v((0, 512)).keys)
            TR(PS(6, 0, 128, part=(0, 72)), V(hist.ap[:, l, :, :].rearrange("p a b -> p (a b)"), hist.v(l, None, None).keys), ident)
            CP("dve", V(tmix.ap[0:72, 0, :], tmix.v(0, None).keys), PS(6, 0, 128, part=(0, 72)))
            for kq in range(24):
                DMA("sp", convp_d[l][:, kq * 128:(kq + 1) * 128], tmix.ap[kq * 3:kq * 3 + 3, 0, :], "fin",
                    reads=tmix.v(0, None).keys)
        p.tag = f"t{t}l{l}:uv"
        for s in range(2):
            sb = slab_next()
            for jj in range(4):
                j = 4 * s + jj
                b = nextbank()
                for k in range(KC):
                    MM(PS(b), slab_v(sb, KC, 512, k, jj * 128, jj * 128 + 128), hT.v(k, (0, T)),
                       start=(k == 0), stop=(k == KC - 1))
                ACT(uT.v(j, (0, T)), PS(b), AF.Gelu)
                if smp_in:
                    smp_ride_fm(sb, KC, 512, jj * 128, lambda k: hT.v(k, SC),
                                lambda pv, j=j: ACT(uT.v(j, SC), pv, AF.Gelu))
                tick_lo()
            slab_done()
        sv = [slab_next(), slab_next()]
        for c in range(NCH):
            q = c % 2
            MEMSET("dve", vstat.all(), 0.0)
            for s in range(2):
                b = nextbank()
                for k in range(KC):
                    MM(PS(b), hT.v(k, (c * 128, c * 128 + 128)), slab_v(sv[s], KC, 512, k, 0, 512),
                       start=(k == 0), stop=(k == KC - 1))
                ACT(gv[q].v((s * 512, s * 512 + 512)), PS(b), AF.Gelu, accum=vstat.v((s, s + 1)))
            ACT(junk2.all(), gv[q].all(), AF.Square, accum=vstat.v((2, 3)))
            TT("dve", vstat.v((3, 4)), vstat.v((0, 1)), vstat.v((1, 2)), ALU.add)
            TS("dve", vstat.v((3, 4)), vstat.v((3, 4)), 1.0 / D, None, ALU.mult)
            TT("dve", vstat.v((4, 5)), vstat.v((3, 4)), vstat.v((3, 4)), ALU.mult)
            STT("dve", vstat.v((5, 6)), vstat.v((2, 3)), 1.0 / D, vstat.v((4, 5)), ALU.mult, ALU.subtract)
            ACT(vstat.v((6, 7)), vstat.v((5, 6)), AF.Sqrt, bias=eps_c, scale=1.0)
            p.add("dve", lambda e, o=vstat.ap[:, 7:8], i=vstat.ap[:, 6:7]: e.reciprocal(out=o, in_=i),
                  reads=vstat.all().keys, writes=vstat.all().keys)
            TS("dve", vln.v(c, None), gv[q].all(), vstat.v((3, 4)), vstat.v((7, 8)), ALU.subtract, ALU.mult)
        if smp_in:
            g16 = V(gv[0].ap[0:NSMP, :], gv[0].all().keys)
            vs16 = lambda a, b_: V(vstat.ap[0:NSMP, a:b_], vstat.all().keys)
            MEMSET("dve", vstat.all(), 0.0)
            DMA("sp", lnvg_s.ap[0:NSMP, :], lnvbc_d[:, (l * 2) * D:(l * 2 + 1) * D], "sg1", writes=lnvg_s.all().keys)
            DMA("sp", lnvb_s.ap[0:NSMP, :], lnvbc_d[:, (l * 2 + 1) * D:(l * 2 + 2) * D], "sg2", writes=lnvb_s.all().keys)
            for s in range(2):
                b = sbank()
                for k in range(KC):
                    MM(PS(b, 0, 512, part=(0, NSMP)), hT.v(k, SC), slab_v(sv[s], KC, 512, k, 0, 512),
                       start=(k == 0), stop=(k == KC - 1))
                ACT(V(gv[0].ap[0:NSMP, s * 512:(s + 1) * 512], gv[0].v((s * 512, s * 512 + 512)).keys),
                    PS(b, 0, 512, part=(0, NSMP)), AF.Gelu, accum=vs16(s, s + 1))
            ACT(V(junk2.ap[0:NSMP, :], junk2.all().keys), g16, AF.Square, accum=vs16(2, 3))
            TT("dve", vs16(3, 4), vs16(0, 1), vs16(1, 2), ALU.add)
            TS("dve", vs16(3, 4), vs16(3, 4), 1.0 / D, None, ALU.mult)
            TT("dve", vs16(4, 5), vs16(3, 4), vs16(3, 4), ALU.mult)
            STT("dve", vs16(5, 6), vs16(2, 3), 1.0 / D, vs16(4, 5), ALU.mult, ALU.subtract)
            ACT(vs16(6, 7), vs16(5, 6), AF.Sqrt, bias=V(smalls.ap[0:NSMP, 1:2], smalls.all().keys), scale=1.0)
            p.add("dve", lambda e, o=vstat.ap[0:NSMP, 7:8], i=vstat.ap[0:NSMP, 6:7]: e.reciprocal(out=o, in_=i),
                  reads=vstat.all().keys, writes=vstat.all().keys)
            TS("dve", g16, g16, vs16(3, 4), vs16(7, 8), ALU.subtract, ALU.mult)
            TT("dve", g16, g16, V(lnvg_s.ap[0:NSMP, :], lnvg_s.all().keys), ALU.mult)
            TT("dve", g16, g16, V(lnvb_s.ap[0:NSMP, :], lnvb_s.all().keys), ALU.add)
            DMA("sp", vs_d[l], g16.ap, "svs", reads=g16.keys)
            for k in range(KC):
                TR(PS(4, k * NSMP, k * NSMP + NSMP), V(gv[0].ap[0:NSMP, k * 128:(k + 1) * 128], gv[0].all().keys),
                   V(cf32.ap[0:NSMP, 0, 0:NSMP], ident.keys))
            CP("dve", vlnTs.all(), PS(4, 0, KC * NSMP), in_ap=PSA[:, 4, 0:KC * NSMP].rearrange("p (a b) -> p a b", a=KC))
        slab_done()
        slab_done()
        tick_lo(4)
        p.tag = f"t{t}l{l}:mix"
        def mix_chunk(c):
            for g in range(8):
                hf, gg = divmod(g, 4)
                MM(PS(6 + hf, gg * 128, gg * 128 + 128), vln.v(c, (g * 128, g * 128 + 128)), wsTb.v(l, g, None))
            for g in range(8):
                hf, gg = divmod(g, 4)
                STT("dve", tmix.v(g, None), PS(6 + hf, gg * 128, gg * 128 + 128), pcol.v(l, (LNVG + g, LNVG + g + 1)),
                    BIASb.v(l, g, None), ALU.mult, ALU.add)
            TT("pool", uT.v(None, (c * 128, c * 128 + 128)), tmix.all(), uT.v(None, (c * 128, c * 128 + 128)), ALU.mult)
        if smp_in:
            TT("dve", stmp.all(), vlnTs.all(), pcol.v(l, (WDIAG, WDIAG + 8)), ALU.mult,
               in1_ap=bc_last(pcol.v(l, (WDIAG, WDIAG + 8)), NSMP))
            TT("dve", stmp.all(), stmp.all(), pcol.v(l, (BS0, BS0 + 8)), ALU.add,
               in1_ap=bc_last(pcol.v(l, (BS0, BS0 + 8)), NSMP))
            TT("dve", ocmS.all(), stmp.all(), uT.v(None, SC), ALU.mult)
        p.tag = f"t{t}l{l}:merge"
        def gates(br):
            for s in range(2):
                sb = slab_next()
                for jj in range(4):
                    j = 4 * s + jj
                    b = nextbank()
                    for k in range(KC):
                        MM(PS(b), slab_v(sb, KC, 512, k, jj * 128, jj * 128 + 128), hT.v(k, (0, T)),
                           start=(k == 0), stop=(k == KC - 1))
                    ACT(sg.v(br, j, (0, T)), PS(b), AF.Sigmoid,
                        bias=pcol.v(l, (BGATE + br * 8 + j, BGATE + br * 8 + j + 1)))
                    if (br * 8 + j) % 3 == 0 and mixq:
                        mix_chunk(mixq.pop(0))
                    tick_lo()
                    if smp_in:
                        smp_ride_fm(sb, KC, 512, jj * 128, lambda k: hT.v(k, SC),
                                    lambda pv, j=j, br=br: ACT(sgS.v(br, j, None), pv, AF.Sigmoid,
                                                               bias=pcol.v(l, (BGATE + br * 8 + j, BGATE + br * 8 + j + 1))))
                slab_done()

        def branch(br, first):
            nk, cw, nsl, nb = (KC, 512, 2, 4) if br == 0 else (16, 256, 4, 2)
            for s in range(nsl):
                sb = slab_next()
                for jj in range(nb):
                    j = nb * s + jj
                    b = nextbank()
                    for k in range(nk):
                        rhs = uT.v(k, (0, T)) if br == 0 else yznT.v(k, None)
                        MM(PS(b), slab_v(sb, nk, cw, k, jj * 128, jj * 128 + 128), rhs, start=(k == 0), stop=(k == nk - 1))
                    q = j % 2
                    if first:
                        TT("dve", mg.v(j, (0, T)), PS(b), sg.v(br, j, (0, T)), ALU.mult)
                    else:
                        TT("dve", tmpf[q].all(), PS(b), sg.v(br, j, (0, T)), ALU.mult)
                        TT("pool", mg.v(j, (0, T)), tmpf[q].all(), mg.v(j, (0, T)), ALU.add)
                    if smp_out:
                        def ev(pv, j=j):
                            if first:
                                TT("dve", mg.v(j, SC), pv, sgS.v(br, j, None), ALU.mult)
                            else:
                                TT("dve", stmp.v(0, None), pv, sgS.v(br, j, None), ALU.mult)
                                TT("dve", mg.v(j, SC), stmp.v(0, None), mg.v(j, SC), ALU.add)
                        rfn = (lambda k: ocmS.v(k, None)) if br == 0 else (lambda k: yznTs.v(k, None))
                        smp_ride_fm(sb, nk, cw, jj * 128, rfn, ev)
                    tick_lo()
                slab_done()

        mixq = list(range(NCH))
        gates(0)
        gates(1)
        while mixq:
            mix_chunk(mixq.pop(0))
        if smp_out:
            tick(1000)
        branch(1, True)
        branch(0, False)
        p.tag = f"t{t}l{l}:wo"
        for s in range(2):
            sb = slab_next()
            for jj in range(4):
                j = 4 * s + jj
                b = nextbank()
                for k in range(KC):
                    MM(PS(b), slab_v(sb, KC, 512, k, jj * 128, jj * 128 + 128), mg.v(k, (0, T)),
                       start=(k == 0), stop=(k == KC - 1))
                STT("dve", xT.v(j, (0, T)), PS(b), modc(l, 16 + j), xT.v(j, (0, T)), ALU.mult, ALU.add)
                if smp_out:
                    def ev_wo(pv, j=j):
                        TT("dve", stmp.v(0, None), pv, mod.v(l, 16 + j, (1, 17)), ALU.mult)
                        TT("dve", xT.v(j, SC), stmp.v(0, None), xT.v(j, SC), ALU.add)
                    smp_ride_fm(sb, KC, 512, jj * 128, lambda k: mg.v(k, SC), ev_wo)
                tick_lo()
            slab_done()
        layer_norm_apply(l, 1, smp_out)
        p.tag = f"t{t}l{l}:ffn"
        for jb in range(6):
            sbg = slab_next()
            sbu = slab_next()
            nblk = 4 if jb < 5 else 2
            for jj in range(nblk):
                j = 4 * jb + jj
                bg = nextbank()
                for k in range(KC):
                    MM(PS(bg), slab_v(sbg, KC, 512, k, jj * 128, jj * 128 + 128), hT.v(k, (0, T)),
                       start=(k == 0), stop=(k == KC - 1))
                bu = nextbank()
                for k in range(KC):
                    MM(PS(bu), slab_v(sbu, KC, 512, k, jj * 128, jj * 128 + 128), hT.v(k, (0, T)),
                       start=(k == 0), stop=(k == KC - 1))
                q = j % 2
                ACT(sgt[q].v((0, T)), PS(bg), AF.Silu)
                TT("dve", actb.v(j, (0, T)), PS(bu), sgt[q].v((0, T)), ALU.mult)
                if smp_out:
                    for k in range(KC):
                        MM(PS(4, 0, NSMP), slab_v(sbg, KC, 512, k, jj * 128, jj * 128 + 128), hT.v(k, SC),
                           start=(k == 0), stop=(k == KC - 1))
                    for k in range(KC):
                        MM(PS(5, 0, NSMP), slab_v(sbu, KC, 512, k, jj * 128, jj * 128 + 128), hT.v(k, SC),
                           start=(k == 0), stop=(k == KC - 1))
                    ACT(sgt[q].v(SC), PS(4, 0, NSMP), AF.Silu)
                    TT("dve", actb.v(j, SC), PS(5, 0, NSMP), sgt[q].v(SC), ALU.mult)
                tick()
            slab_done()
            slab_done()
        for cg in range(4):
            b0 = nextbank()
            b1 = nextbank()
            bb = (b0, b1)
            for kh in range(2):
                sb = slab_next()
                for jj in range(2):
                    for k in range(11):
                        kk = kh * 11 + k
                        MM(PS(bb[jj]), slab_v(sb, 11, 256, k, jj * 128, jj * 128 + 128), actb.v(kk, (0, T)),
                           start=(kk == 0), stop=(kk == KFF - 1))
                    if smp_out:
                        for k in range(11):
                            kk = kh * 11 + k
                            MM(PS(4 + jj, 0, NSMP), slab_v(sb, 11, 256, k, jj * 128, jj * 128 + 128), actb.v(kk, SC),
                               start=(kk == 0), stop=(kk == KFF - 1))
                slab_done()
            for jj in range(2):
                j = cg * 2 + jj
                STT("dve", xT.v(j, (0, T)), PS(bb[jj]), modc(l, 40 + j), xT.v(j, (0, T)), ALU.mult, ALU.add)
                if smp_out:
                    TT("dve", stmp.v(0, None), PS(4 + jj, 0, NSMP), mod.v(l, 40 + j, (1, 17)), ALU.mult)
                    TT("dve", xT.v(j, SC), stmp.v(0, None), xT.v(j, SC), ALU.add)
                tick()
        layer_norm_apply(l, 2, smp_out)
        p.tag = f"t{t}l{l}:out"
        if l == DEPTH - 1 and smp_out:
            for k in range(KC):
                hf, kk = divmod(k, 4)
                TR(PS(4 + hf, kk * 128, kk * 128 + 128, part=(0, NSMP)), xT.v(k, SC), ident)
            for hf in range(2):
                CP("act", V(ystage[0].ap[0:NSMP, hf * 512:(hf + 1) * 512], ystage[0].v((hf * 512, hf * 512 + 512)).keys),
                   PS(4 + hf, 0, 512, part=(0, NSMP)))
            DMA("sp", ys_d, ystage[0].ap[0:NSMP, :], "ys0", reads=ystage[0].all().keys)
        if l == DEPTH - 1:
            for c in range(NCH):
                q = c % 2
                for hf in range(2):
                    for kk in range(4):
                        k = hf * 4 + kk
                        TR(PS(4 + hf, kk * 128, kk * 128 + 128), xT.v(k, (c * 128, c * 128 + 128)), ident)
                    CP("act" if hf == 0 else "dve", ystage[q].v((hf * 512, hf * 512 + 512)), PS(4 + hf))
                r0 = t * T + c * 128
                DMA("sp", yp_d[r0:r0 + 128, :], ystage[q].ap, f"ys{q}", reads=ystage[q].all().keys)

    ystage_a = [Buf(arena, tmpf[0].off, [512], F32), Buf(arena, tmpf[1].off, [512], F32)]

    for t in range(n_tiles):
        for l in range(DEPTH):
            if t == 0 and l == 1:
                p.tag = "mod1"
                compute_mod(1)
            tile_layer(t, l)

    p.emit(sems, dsems)
    st.close()
    return nc, p


def _consts():
    c = np.zeros((128, 6, 128), np.float32)
    c[:, 0, :] = np.eye(128, dtype=np.float32)
    c[:, 1, :] = np.triu(np.ones((128, 128), np.float32))
    c[:, 2, :] = np.tril(np.ones((128, 128), np.float32), -1)
    c[:, 3, :] = 1.0
    q = np.arange(128)
    c[:, 4, :] = (q[:, None] // 8 == q[None, :] // 8).astype(np.float32)
    for hq in range(8):
        for b in range(16):
            c[b * 8 + hq, 5, hq * 16 + b] = 1.0
    return c.reshape(128, 6 * 128)


def _shared_inputs(inp):
    f = lambda k: np.asarray(inp[k], dtype=np.float32)
    w_in = f("w_in")
    wada = np.zeros((DEPTH, 12, 128, SLAB), np.float32)
    wstr = np.zeros((DEPTH, NSLAB, 128, SLAB), np.float32)
    wdt = np.zeros((128, DEPTH, KC, 32), np.float32)
    pcol = np.zeros((128, DEPTH, NPC), np.float32)
    prow = np.zeros((128, DEPTH, 96), np.float32)
    pq = np.zeros((128, DEPTH, 8), np.float32)
    wst = np.zeros((128, DEPTH, 8, 128), np.float32)
    bsb = np.zeros((128, DEPTH, 1024), np.float32)
    lnvbc = np.zeros((NSMP, DEPTH, 2, 1024), np.float32)
    for l in range(DEPTH):
        for s in range(12):
            wada[l, s] = _slab_pack(f("w_ada")[l], s * 512, 512, 0, KC)
        i = 0
        order = [("x", 0), ("z", 0), ("x", 1), ("z", 1), ("x", 2), ("z", 2), ("x", 3), ("z", 3), ("x", 4), ("x", 5),
                 ("u", 0), ("u", 1), ("v", 0), ("v", 1)]
        base = {"x": 4096, "z": 2048, "u": 0, "v": 1024}
        for nm, s_ in order:
            wstr[l, i] = _slab_pack(w_in[l], base[nm] + s_ * 512, 512, 0, KC)
            i += 1
        for s in range(2):
            wstr[l, i] = _slab_pack(w_in[l], 7200 + s * 512, 512, 0, KC); i += 1
        for s in range(2):
            wstr[l, i] = _slab_pack(w_in[l], 8224 + s * 512, 512, 0, KC); i += 1
        for s in range(4):
            wstr[l, i] = _slab_pack(f("w_ssd_br")[l], s * 256, 256, 0, 16); i += 1
        for s in range(2):
            wstr[l, i] = _slab_pack(f("w_cm_br")[l], s * 512, 512, 0, KC); i += 1
        for s in range(2):
            wstr[l, i] = _slab_pack(f("w_o")[l], s * 512, 512, 0, KC); i += 1
        for s in range(6):
            wstr[l, i] = _slab_pack(f("w_ffn_gate")[l], s * 512, 512, 0, KC); i += 1
            wstr[l, i] = _slab_pack(f("w_ffn_up")[l], s * 512, 512, 0, KC); i += 1
        for cg in range(4):
            for kh in range(2):
                wstr[l, i] = _slab_pack(f("w_ffn_down")[l], cg * 256, 256, kh * 11, 11); i += 1
        assert i == NSLAB
        wdt[:, l] = w_in[l][:, 7168:7200].reshape(KC, 128, 32).transpose(1, 0, 2)
        pcol[:, l, BADA:BADA + 48] = _pcols(f("b_ada")[l])
        pcol[:, l, BGATE:BGATE + 16] = _pcols(f("b_gate")[l])
        pcol[:, l, LNVG:LNVG + 8] = _pcols(f("ln_v_g")[l])
        pcol[:, l, LNVB:LNVB + 8] = _pcols(f("ln_v_b")[l])
        for kk in range(4):
            pcol[:, l, CONVW + kk * 24:CONVW + kk * 24 + 24] = _pcols(f("conv_w")[l, kk])
        pcol[:, l, CONVB:CONVB + 24] = _pcols(f("conv_b")[l])
        pcol[:, l, NORMW:NORMW + 16] = _pcols(f("ssd_norm_w")[l])
        pcol[:, l, LN1G:LN1G + 8] = _pcols(f("ln1_g")[l])
        pcol[:, l, LN1B:LN1B + 8] = _pcols(f("ln1_b")[l])
        pcol[:, l, LN2G:LN2G + 8] = _pcols(f("ln2_g")[l])
        pcol[:, l, LN2B:LN2B + 8] = _pcols(f("ln2_b")[l])
        pcol[:, l, WDIAG:WDIAG + 8] = np.broadcast_to(f("w_spatial")[l, :, 0, 0][None, :], (128, 8))
        pcol[:, l, BS0:BS0 + 8] = np.broadcast_to(f("b_spatial")[l, :, 0][None, :], (128, 8))
        prow[:, l, 0:32] = f("dt_bias")[l][None, :]
        prow[:, l, 32:64] = f("a_log")[l][None, :]
        prow[:, l, 64:96] = f("d_skip")[l][None, :]
        hq = np.arange(128) % 8
        for hi in range(4):
            pq[:, l, hi] = f("a_log")[l][hq * 4 + hi]
            pq[:, l, 4 + hi] = f("d_skip")[l][hq * 4 + hi]
        wst[:, l] = f("w_spatial")[l].transpose(2, 0, 1)
        bsb[:, l] = f("b_spatial")[l].reshape(1, 1024)
        lnvbc[:, l, 0] = f("ln_v_g")[l][None, :]
        lnvbc[:, l, 1] = f("ln_v_b")[l][None, :]
    return {
        "wada": wada, "wstr": wstr, "wdt": wdt.reshape(128, -1), "pcol": pcol.reshape(128, -1),
        "prow": prow.reshape(128, -1), "pq": pq.reshape(128, -1), "wst": wst.reshape(128, -1),
        "bsb": bsb.reshape(128, -1), "lnvbc": lnvbc.reshape(NSMP, -1), "consts": _consts(),
    }


_CACHE = {}


def kernel(**inp):
    f = lambda k: np.asarray(inp[k], dtype=np.float32)
    shared = _shared_inputs(inp)
    in_maps = []
    for i in range(NCORES):
        m = dict(shared)
        m["xp"] = np.ascontiguousarray(f("x_prompt")[i])
        m["xs"] = np.ascontiguousarray(f("x_sample")[16 * i:16 * i + 16, 0, :])
        m["cc"] = np.ascontiguousarray(np.concatenate([f("c_prompt")[i:i + 1], f("c_sample")[16 * i:16 * i + 16]], 0))
        m["sssd"] = np.ascontiguousarray(f("state_ssd")[:, 16 * i:16 * i + 16]).reshape(DEPTH, NSMP * 8, 4, HP * NST)
        m["sconv"] = np.ascontiguousarray(f("state_conv")[:, 16 * i:16 * i + 16]).reshape(DEPTH, NSMP * 3, CONV)
        in_maps.append(m)
    if "nc" not in _CACHE:
        _CACHE["nc"] = build_program()[0]
    nc = _CACHE["nc"]
    res = run_bass_kernel_spmd(nc, in_maps, core_ids=list(range(NCORES)))
    R = res.results
    y_prompt = np.stack([R[i]["yp"] for i in range(NCORES)], 0)
    y_sample = np.concatenate([R[i]["ys"] for i in range(NCORES)], 0).reshape(128, 1, D)
    ssd_p = np.stack([R[i]["ssdp"].reshape(DEPTH, NH, HP, NST) for i in range(NCORES)], 1)
    conv_p = np.stack([R[i]["convp"] for i in range(NCORES)], 1)
    ssd_s = np.concatenate([R[i]["ssds"].reshape(DEPTH, NSMP, NH, HP, NST) for i in range(NCORES)], 1)
    conv_s = np.concatenate([R[i]["convs"] for i in range(NCORES)], 1)
    v_s = np.concatenate([R[i]["vs"] for i in range(NCORES)], 1).reshape(DEPTH, 128, 1, D)
    return (y_prompt, y_sample, ssd_p, conv_p, ssd_s, conv_s, v_s)
```
